# Optimizing a Trainium2 kernel written in Bass

```python
import math
import jax, jax.numpy as jnp
from jax import lax
import numpy as np

D_MODEL = 1024
BATCH = 4
SEQ = 4096
DEPTH = 1

N_META = 16
BLOCK_Q = 128
DIFF_HEADS = 8
DIFF_HEAD_DIM = 64
DIFF_V_DIM = 2 * DIFF_HEAD_DIM
DIFF_QK = DIFF_HEADS * 2 * DIFF_HEAD_DIM
DIFF_WIDTH = DIFF_HEADS * DIFF_V_DIM
SB_HEADS = 16
SB_HEAD_DIM = 64
SB_WIDTH = SB_HEADS * SB_HEAD_DIM
IN_SIZES = [DIFF_QK, DIFF_QK, DIFF_WIDTH, DIFF_WIDTH,
            SB_WIDTH, SB_WIDTH, SB_WIDTH, SB_WIDTH,
            D_MODEL, D_MODEL]
IN_OFFSETS = [int(v) for v in np.cumsum(IN_SIZES)[:-1]]
N_IN = int(sum(IN_SIZES))
DN_ALPHA = (2.0 * DEPTH) ** 0.25
DN_BETA = (8.0 * DEPTH) ** -0.25
LN_EPS = 1e-5
RMS_EPS = 1e-5

kernel_name = 'hybrid_diffattn_stickbreaking_gated_merge'


def layer_norm(x, g, b):
    xf = x.astype(jnp.float32)
    mu = jnp.mean(xf, axis=-1, keepdims=True)
    var = jnp.mean(jnp.square(xf - mu), axis=-1, keepdims=True)
    return ((xf - mu) * lax.rsqrt(var + LN_EPS) * g.astype(jnp.float32) + b.astype(jnp.float32)).astype(x.dtype)


def rms_norm(x, g):
    xf = x.astype(jnp.float32)
    y = xf * lax.rsqrt(jnp.mean(jnp.square(xf), axis=-1, keepdims=True) + RMS_EPS)
    return (y * g.astype(jnp.float32)).astype(x.dtype)


def block_bounds(total_len):
    bounds = [(0, N_META)]
    for t0 in range(N_META, total_len, BLOCK_Q):
        bounds.append((t0, min(t0 + BLOCK_Q, total_len)))
    return bounds


def diff_attn_block(q, k, v, q_pos, slopes, lam):
    k_pos = jnp.arange(k.shape[3])
    dist = (q_pos[:, None] - k_pos[None, :])
    s = jnp.einsum('bhcqd,bhckd->bhcqk', q, k).astype(jnp.float32) * (DIFF_HEAD_DIM ** -0.5)
    alibi = -slopes[:, None, None] * dist.astype(jnp.float32)[None]
    s = jnp.where(dist >= 0, s + alibi[None, :, None], -jnp.inf)
    p = jax.nn.softmax(s, axis=-1)
    a = p[:, :, 0] - lam * p[:, :, 1]
    return jnp.einsum('bhqk,bhkd->bhqd', a.astype(v.dtype), v)


def stick_breaking_block(q, k, v, q_pos):
    k_pos = jnp.arange(k.shape[2])
    before = k_pos[None, :] < q_pos[:, None]
    z = jnp.einsum('bhqd,bhkd->bhqk', q, k).astype(jnp.float32) * (SB_HEAD_DIM ** -0.5)
    log_keep = jnp.where(before, jax.nn.log_sigmoid(-z), 0.0)
    suffix = lax.cumsum(log_keep, axis=3, reverse=True) - log_keep
    a = jnp.where(before, jnp.exp(jax.nn.log_sigmoid(z) + suffix), 0.0)
    return jnp.einsum('bhqk,bhkd->bhqd', a.astype(v.dtype), v)


def setup_inputs(seed: int = 0) -> dict:
    key = jax.random.key(seed)
    ks = jax.random.split(key, 14)
    f32 = jnp.float32
    x = jax.random.normal(ks[0], (BATCH, SEQ, D_MODEL), f32)
    meta_tokens = jax.random.normal(ks[1], (N_META, D_MODEL), f32)
    emb_ln_g = 1.0 + 0.02 * jax.random.normal(ks[2], (D_MODEL,), f32)
    emb_ln_b = 0.02 * jax.random.normal(ks[3], (D_MODEL,), f32)
    col_scale = np.concatenate([np.full((n,), DN_BETA if i in (2, 6) else 1.0, np.float32)
                                for i, n in enumerate(IN_SIZES)])
    w_in = jax.random.normal(ks[4], (DEPTH, D_MODEL, N_IN), f32) * (D_MODEL ** -0.5) * jnp.asarray(col_scale)
    b_gate = 0.01 * jax.random.normal(ks[5], (DEPTH, 2, D_MODEL), f32)
    diff_lambda = 0.1 * jax.random.normal(ks[6], (DEPTH, 4, DIFF_HEAD_DIM), f32)
    diff_subln_g = 1.0 + 0.02 * jax.random.normal(ks[7], (DEPTH, DIFF_V_DIM), f32)
    w_br_diff = jax.random.normal(ks[8], (DEPTH, DIFF_WIDTH, D_MODEL), f32) * (DIFF_WIDTH ** -0.5) * DN_BETA
    w_br_sb = jax.random.normal(ks[9], (DEPTH, SB_WIDTH, D_MODEL), f32) * (SB_WIDTH ** -0.5) * DN_BETA
    w_out = jax.random.normal(ks[10], (DEPTH, D_MODEL, D_MODEL), f32) * (D_MODEL ** -0.5) * DN_BETA
    ln_g = 1.0 + 0.02 * jax.random.normal(ks[11], (DEPTH, D_MODEL), f32)
    ln_b = 0.02 * jax.random.normal(ks[12], (DEPTH, D_MODEL), f32)
    return {'x': x, 'meta_tokens': meta_tokens, 'emb_ln_g': emb_ln_g, 'emb_ln_b': emb_ln_b,
            'w_in': w_in, 'b_gate': b_gate, 'diff_lambda': diff_lambda, 'diff_subln_g': diff_subln_g,
            'w_br_diff': w_br_diff, 'w_br_sb': w_br_sb, 'w_out': w_out, 'ln_g': ln_g, 'ln_b': ln_b}


def reference(x, meta_tokens, emb_ln_g, emb_ln_b, w_in, b_gate, diff_lambda, diff_subln_g,
              w_br_diff, w_br_sb, w_out, ln_g, ln_b):
    B = x.shape[0]
    meta = jnp.broadcast_to(meta_tokens.astype(x.dtype)[None], (B, N_META, D_MODEL))
    h = jnp.concatenate([meta, x], axis=1)
    L = h.shape[1]
    h = layer_norm(h, emb_ln_g, emb_ln_b)
    bounds = block_bounds(L)
    slopes = 2.0 ** (-8.0 * jnp.arange(1, DIFF_HEADS + 1, dtype=jnp.float32) / DIFF_HEADS)

    for i in range(DEPTH):
        lam_init = 0.8 - 0.6 * math.exp(-0.3 * i)
        lp = diff_lambda[i].astype(jnp.float32)
        lam = jnp.exp(jnp.sum(lp[0] * lp[1])) - jnp.exp(jnp.sum(lp[2] * lp[3])) + lam_init

        proj = jnp.einsum('bld,dn->bln', h, w_in[i])
        q_d, k_d, v_d, g_d, q_s, k_s, v_s, g_s, m_d, m_s = jnp.split(proj, IN_OFFSETS, axis=-1)
        q_d = q_d.reshape(B, L, DIFF_HEADS, 2, DIFF_HEAD_DIM).transpose(0, 2, 3, 1, 4)
        k_d = k_d.reshape(B, L, DIFF_HEADS, 2, DIFF_HEAD_DIM).transpose(0, 2, 3, 1, 4)
        v_d = v_d.reshape(B, L, DIFF_HEADS, DIFF_V_DIM).transpose(0, 2, 1, 3)
        q_s = q_s.reshape(B, L, SB_HEADS, SB_HEAD_DIM).transpose(0, 2, 1, 3)
        k_s = k_s.reshape(B, L, SB_HEADS, SB_HEAD_DIM).transpose(0, 2, 1, 3)
        v_s = v_s.reshape(B, L, SB_HEADS, SB_HEAD_DIM).transpose(0, 2, 1, 3)

        outs_d, outs_s = [], []
        for (t0, t1) in bounds:
            q_pos = jnp.arange(t0, t1)
            outs_d.append(diff_attn_block(q_d[:, :, :, t0:t1], k_d[:, :, :, :t1], v_d[:, :, :t1],
                                          q_pos, slopes, lam))
            outs_s.append(stick_breaking_block(q_s[:, :, t0:t1], k_s[:, :, :t1], v_s[:, :, :t1], q_pos))
        y_d = jnp.concatenate(outs_d, axis=2)
        y_s = jnp.concatenate(outs_s, axis=2)

        y_d = rms_norm(y_d, diff_subln_g[i]) * (1.0 - lam_init)
        y_d = y_d.transpose(0, 2, 1, 3).reshape(B, L, DIFF_WIDTH)
        y_s = y_s.transpose(0, 2, 1, 3).reshape(B, L, SB_WIDTH)

        br_d = jnp.einsum('blw,wd->bld', y_d * jax.nn.silu(g_d), w_br_diff[i])
        br_s = jnp.einsum('blw,wd->bld', y_s * jax.nn.silu(g_s), w_br_sb[i])
        merged = jax.nn.sigmoid(m_d + b_gate[i, 0]) * br_d + jax.nn.sigmoid(m_s + b_gate[i, 1]) * br_s
        y = jnp.einsum('bld,de->ble', merged, w_out[i])

        h = layer_norm(DN_ALPHA * h + y, ln_g[i], ln_b[i])

    return h[:, N_META:].astype(x.dtype)
```

```python
import numpy as np
import ml_dtypes
import concourse.bass as bass
import concourse.mybir as mybir
from concourse.bass_utils import run_bass_kernel_spmd

F32 = mybir.dt.float32
BF16 = mybir.dt.bfloat16
AF = mybir.ActivationFunctionType
ALU = mybir.AluOpType

D = 1024
SEQ = 4096
NMETA = 16
L = SEQ + NMETA
NCORES = 8
NEG = -30000.0
LN_EPS = 1e-5
RMS_EPS = 1e-5
DN_ALPHA = 2.0 ** 0.25
LAM_INIT = 0.8 - 0.6 * 1.0
SLOPES = [2.0 ** (-(h + 1)) for h in range(8)]


class _Op:
    __slots__ = ("fn", "eng", "lane", "lidx", "deps", "dma", "signal")


class Tracker:
    COMPUTE = ("pe", "act", "dve", "pool")

    def __init__(self):
        self.streams = {e: [] for e in ("pe", "act", "dve", "pool", "sp")}
        self.lane_ops = {}
        self.last_w = {}
        self.readers = {}

    def _add(self, eng, lane, fn, reads, writes, dma):
        op = _Op()
        op.fn, op.eng, op.lane, op.dma, op.signal = fn, eng, lane, dma, dma
        lst = self.lane_ops.setdefault(lane, [])
        lst.append(op)
        op.lidx = len(lst)
        deps = {}

        def add(d, raw=False):
            if d is None:
                return
            if d.lane == lane and not dma and lane == "pe":
                return
            cur = deps.get(d.lane)
            if cur is None or cur.lidx < d.lidx:
                deps[d.lane] = d

        for k in reads:
            add(self.last_w.get(k), True)
        for k in writes:
            add(self.last_w.get(k))
            for r in self.readers.get(k, {}).values():
                add(r)
        if dma and len(lst) > 1:
            add(lst[-2])
        op.deps = list(deps.values())
        for k in reads:
            self.readers.setdefault(k, {})[lane] = op
        for k in writes:
            self.last_w[k] = op
            self.readers[k] = {}
        self.streams[eng].append(op)
        return op

    def op(self, eng, fn, reads=(), writes=()):
        return self._add(eng, eng, fn, reads, writes, False)

    def dma(self, eng, lane, fn, reads=(), writes=()):
        return self._add(eng, lane, fn, reads, writes, True)

    def emit(self, nc, sems, block):
        for e, ops in self.streams.items():
            for op in ops:
                for d in op.deps:
                    d.signal = True
        val = {}
        for lane, ops in self.lane_ops.items():
            c = 0
            for op in ops:
                if op.signal:
                    c += 16 if op.dma else 1
                val[(lane, op.lidx)] = c
        final = {lane: val[(lane, len(ops))] for lane, ops in self.lane_ops.items()
                 if ops and ops[-1].dma}
        streams = self.streams

        def run(engname, eng):
            waited = {}
            for op in streams[engname]:
                for d in op.deps:
                    v = val[(d.lane, d.lidx)]
                    if waited.get(d.lane, 0) < v:
                        eng.wait_ge(sems[d.lane], v)
                        waited[d.lane] = v
                ins = op.fn(eng)
                if op.signal:
                    ins.then_inc(sems[op.lane], 16 if op.dma else 1)
            if engname == "sp":
                for lane, v in final.items():
                    eng.wait_ge(sems[lane], v)

        @block.tensor
        def _(e):
            run("pe", e)

        @block.scalar
        def _(e):
            run("act", e)

        @block.vector
        def _(e):
            run("dve", e)

        @block.gpsimd
        def _(e):
            run("pool", e)

        @block.sync
        def _(e):
            run("sp", e)


class Arena:
    def __init__(self, base_ap, nwords):
        self.base = base_ap
        self.n = nwords
        self.top = 0

    def mark(self):
        return self.top

    def release(self, m):
        self.top = m

    def alloc(self, nelem, dtype=F32, parts=128):
        if dtype == F32:
            w = nelem
        else:
            w = (nelem + 1) // 2
        w = (w + 7) // 8 * 8
        a = self.top
        self.top += w
        assert self.top <= self.n, f"arena overflow {self.top} > {self.n}"
        ap = self.base[0:parts, a:a + w]
        if dtype != F32:
            ap = ap.bitcast(dtype)
        return ap[:, 0:nelem]


def build_program(stage="full", units=None, passes=(0, 1)):
    nc = bass.Bass("TRN2", target_bir_lowering=False)
    T = Tracker()
    if units is None:
        units = list(range(16))
    AX = mybir.AxisListType.X

    def din(name, shape, dt=F32):
        return nc.dram_tensor(name, list(shape), dt, kind="ExternalInput").ap()

    xf = din("xf", [SEQ, D])
    xq = din("xq", [SEQ // 2, D])
    meta = din("meta", [NMETA, D])
    lnv = din("lnv", [4, D])
    wu = din("wu", [16, 128, 8 * 512])
    wt = din("wt", [8, 128, 4096])
    wo = din("wo", [128, 8 * 1024])
    bgate = din("bgate", [128, 16])
    lamv = din("lamv", [1, 256])
    subg = din("subg", [128, 1])
    cmask = din("cmask", [128, 2 * 8 * 128], BF16)
    cbias = din("cbias", [128, 8 * 36])
    caug = din("caug", [8, 512], BF16)
    cmat = din("cmat", [128, 4 * 128], BF16)
    out = nc.dram_tensor("out", [SEQ // 2, D], F32, kind="ExternalOutput").ap()
    dbg = None
    if stage != "full":
        dbg = nc.dram_tensor("dbg", [128, 8192], F32, kind="ExternalOutput").ap()

    NW = 53000
    lanes = ["pe", "act", "dve", "pool", "dx0", "dx1", "dw0", "dw1", "dw2", "dc", "do0", "do1", "dq"]
    import contextlib
    with contextlib.ExitStack() as es:
        arena_t = es.enter_context(nc.sbuf_tensor("arena", [128, NW], F32))
        ps_t = es.enter_context(nc.psum_tensor("ps", [128, 8 * 512], F32))
        sems = {ln: es.enter_context(nc.semaphore("s_" + ln)) for ln in lanes}
        block = es.enter_context(nc.Block())
        A = Arena(arena_t[:], NW)
        PS = ps_t[:]

        def bank(b):
            return PS[:, b * 512:(b + 1) * 512]

        def PK(b):
            return ("ps", b)

        def mm(out_, lhsT, rhs, start, stop, reads, writes):
            T.op("pe", lambda e: e.matmul(out=out_, lhsT=lhsT, rhs=rhs, start=start, stop=stop),
                 reads=reads, writes=writes)

        def tr(out_, in_, idn, reads, writes):
            T.op("pe", lambda e: e.transpose(out=out_, in_=in_, identity=idn), reads=reads, writes=writes)

        def act(out_, in_, func, reads, writes, bias=None, scale=None):
            kw = {}
            if bias is not None:
                kw["bias"] = bias
            if scale is not None:
                kw["scale"] = scale
            T.op("act", lambda e: e.activation(out=out_, in_=in_, func=func, **kw), reads=reads, writes=writes)

        def acopy(out_, in_, reads, writes):
            T.op("act", lambda e: e.copy(out=out_, in_=in_), reads=reads, writes=writes)

        def vcopy(eng, out_, in_, reads, writes):
            T.op(eng, lambda e: e.tensor_copy(out=out_, in_=in_), reads=reads, writes=writes)

        def tt(eng, out_, in0, in1, op, reads, writes):
            T.op(eng, lambda e: e.tensor_tensor(out=out_, in0=in0, in1=in1, op=op), reads=reads, writes=writes)

        def ts(eng, out_, in0, s1, s2, op0, op1, reads, writes):
            T.op(eng, lambda e: e.tensor_scalar(out=out_, in0=in0, scalar1=s1, scalar2=s2, op0=op0, op1=op1),
                 reads=reads, writes=writes)

        def tsadd(eng, out_, in0, s1, reads, writes):
            T.op(eng, lambda e: e.tensor_scalar_add(out=out_, in0=in0, scalar1=s1), reads=reads, writes=writes)

        def tsmul(eng, out_, in0, s1, reads, writes):
            T.op(eng, lambda e: e.tensor_scalar_mul(out=out_, in0=in0, scalar1=s1), reads=reads, writes=writes)

        def stt(eng, out_, in0, scalar, in1, op0, op1, reads, writes):
            T.op(eng, lambda e: e.scalar_tensor_tensor(out=out_, in0=in0, scalar=scalar, in1=in1, op0=op0, op1=op1),
                 reads=reads, writes=writes)

        def recip(out_, in_, reads, writes):
            T.op("dve", lambda e: e.reciprocal(out=out_, in_=in_), reads=reads, writes=writes)

        def mset(eng, ap, val, reads, writes):
            T.op(eng, lambda e: e.memset(ap, val), reads=reads, writes=writes)

        def dma(eng, lane, out_, in_, reads, writes):
            if eng == "pool":
                T.dma(eng, lane, lambda e: e.dma_start(out=out_, in_=in_, max_dma_last_dim=4096),
                      reads=reads, writes=writes)
            else:
                T.dma(eng, lane, lambda e: e.dma_start(out=out_, in_=in_), reads=reads, writes=writes)

        pbctr = [0]

        def nextbank():
            pbctr[0] = (pbctr[0] + 1) % 8
            return pbctr[0]

        evctr = [0]

        def evac(out_ap, in_ap, reads, writes):
            evctr[0] += 1
            if evctr[0] % 2:
                acopy(out_ap, in_ap, reads, writes)
            else:
                vcopy("dve", out_ap, in_ap, reads, writes)

        cm = A.alloc(512, BF16)
        ident, negtri, negones, ones = (cm[:, i * 128:(i + 1) * 128] for i in range(4))
        masks = A.alloc(2048, BF16)
        biasT = A.alloc(8 * 36, F32)
        gb = A.alloc(4 * D, F32)
        bg = A.alloc(16, F32)
        sg = A.alloc(8, F32)
        onesf = A.alloc(128, F32)
        lamt = A.alloc(640, F32)
        neglam = A.alloc(8, F32)
        dummy = A.alloc(8, F32)
        dma("sp", "dc", cm, cmat, [], ["cm"])
        dma("sp", "dc", masks, cmask, [], ["masks"])
        dma("sp", "dc", biasT, cbias, [], ["biasT"])
        for i in range(4):
            dma("sp", "dc", gb[:, i * D:(i + 1) * D], lnv[i:i + 1, :].partition_broadcast(128), [], [("gb", i)])
        dma("sp", "dc", bg, bgate, [], ["bg"])
        dma("sp", "dc", sg[:, 0:1], subg, [], ["sg"])
        l0 = lamt[0:1, :]
        dma("sp", "dc", l0[:, 0:256], lamv, [], ["lamt"])
        mset("pool", onesf, 1.0 / 128.0, [], ["onesf"])
        tsmul("dve", sg[:, 1:2], sg[:, 0:1], 1.0 - LAM_INIT, ["sg"], ["sgs"])
        tsmul("dve", bg, bg, -1.0, ["bg"], ["bg"])
        tt("dve", l0[:, 256:320], l0[:, 0:64], l0[:, 64:128], ALU.mult, ["lamt"], ["lam1"])
        tt("dve", l0[:, 320:384], l0[:, 128:192], l0[:, 192:256], ALU.mult, ["lamt"], ["lam2"])
        T.op("dve", lambda e: e.reduce_sum(out=l0[:, 384:385], in_=l0[:, 256:320], axis=AX),
             reads=["lam1"], writes=["lam3a"])
        T.op("dve", lambda e: e.reduce_sum(out=l0[:, 385:386], in_=l0[:, 320:384], axis=AX),
             reads=["lam2"], writes=["lam3b"])
        act(l0[:, 386:388], l0[:, 384:386], AF.Exp, ["lam3a", "lam3b"], ["lam4"])
        stt("dve", l0[:, 388:389], l0[:, 387:388], -LAM_INIT, l0[:, 386:387], ALU.add, ALU.subtract,
            ["lam4"], ["lam5"])
        mset("pool", l0[:, 400:528], 1.0, [], ["one1"])
        mm(bank(0)[:, 0:1], l0[:, 400:528], l0[:, 388:389], True, True, ["one1", "lam5"], [PK(0)])
        vcopy("dve", neglam[:, 0:1], bank(0)[:, 0:1], [PK(0)], ["neglam"])

        hT = A.alloc(8 * L, BF16).rearrange("p (c t) -> p c t", c=8)
        NR = 10
        Rf = [A.alloc(512, F32) for _ in range(NR)]

        def RK(i):
            return [("R", i, 0), ("R", i, 1)]

        def Rbf(i, half):
            return Rf[i].bitcast(BF16)[:, half * 512:(half + 1) * 512]

        xbuf = [(Rf[0], Rf[1]), (Rf[2], Rf[3])]
        stat = [A.alloc(16, F32), A.alloc(16, F32)]
        lncount = [0]

        def ln_core(i, rows, gi, out_tiles, out_keys):
            xs = (xbuf[i][0][0:rows], xbuf[i][1][0:rows])
            xk = RK(2 * i) + RK(2 * i + 1)
            st = stat[i][0:rows]
            sk = ("st", i)
            for hf in range(2):
                xh = xs[hf]
                so = st[:, 6 * hf:6 * hf + 6]
                T.op("dve", (lambda xh, so: (lambda e: e.bn_stats(out=so, in_=xh)))(xh, so), reads=xk, writes=[sk])
            T.op("dve", lambda e: e.bn_aggr(out=st[:, 12:14], in_=st[:, 0:12]), reads=[sk], writes=[sk])
            tsadd("dve", st[:, 15:16], st[:, 13:14], LN_EPS, [sk], [sk])
            act(st[:, 15:16], st[:, 15:16], AF.Ln, [sk], [sk])
            act(st[:, 14:15], st[:, 15:16], AF.Exp, [sk], [sk], scale=-0.5)
            for hf in range(2):
                hk = RK(2 * i + hf)
                ts("dve", xs[hf], xs[hf], st[:, 12:13], st[:, 14:15], ALU.subtract, ALU.mult, xk + [sk], hk)
                tt("pool", xs[hf], xs[hf], gb[0:rows, gi * D + hf * 512: gi * D + hf * 512 + 512], ALU.mult,
                   hk + [("gb", gi)], hk)
                tt("dve", out_tiles[hf], xs[hf],
                   gb[0:rows, (gi + 1) * D + hf * 512:(gi + 1) * D + hf * 512 + 512], ALU.add,
                   hk + [("gb", gi + 1)], out_keys[hf])

        def load_x(i, src_ap, rows):
            dma("sp", "dx%d" % i, xbuf[i][0][0:rows], src_ap[:, 0:512], [], RK(2 * i))
            dma("sp", "dx%d" % i, xbuf[i][1][0:rows], src_ap[:, 512:1024], [], RK(2 * i + 1))

        def ln_block_T(src_ap, rows, dstT, col0, key):
            i = lncount[0] % 2
            lncount[0] += 1
            load_x(i, src_ap, rows)
            hb = Rf[4 + i].bitcast(BF16)[0:rows]
            ln_core(i, rows, 0, (hb[:, 0:512], hb[:, 512:1024]), ([("R", 4 + i, 0)], [("R", 4 + i, 1)]))
            pb = nextbank()
            pst = bank(pb).bitcast(BF16).rearrange("p (c t) -> p c t", c=8)
            for c in range(8):
                tr(pst[:, c, 0:rows], hb[:, c * 128:(c + 1) * 128], ident[0:rows, 0:rows],
                   RK(4 + i) + ["cm"], [PK(pb)])
            evac(dstT[:, :, col0:col0 + rows], pst[:, :, 0:rows], [PK(pb)], key if isinstance(key, list) else [key])

        def join(keys, name, slot):
            mset("dve", dummy[:, slot:slot + 1], 0.0, keys, [name + "_d"])
            acopy(dummy[:, slot + 1:slot + 2], dummy[:, slot:slot + 1], keys + [name + "_d"], [name])

        ln_block_T(meta, NMETA, hT, 0, ("hT", 0))
        for g in range(32):
            ln_block_T(xf[g * 128:(g + 1) * 128, :], 128, hT, NMETA + g * 128, ("hT", 1 + g))
        join([("hT", k) for k in range(33)], "hTall", 0)

        hqT = A.alloc(8 * 1024, BF16).rearrange("p (c t) -> p c t", c=8)
        YT = [A.alloc(1024, BF16) for _ in range(16)]
        mark_unit = A.mark()

        def unit_proj(P, u, B):
            is_diff = u < 8
            h = u % 8
            wb, wb3 = B["wb"], B["wb3"]
            KTa, KTb, V3, QTa, QTb, GT = B["KTa"], B["KTb"], B["V3"], B["QTa"], B["QTb"], B["GT"]
            nreal, ncols, ktiles, KTkeys, Vkeys = B["nreal"], B["ncols"], B["ktiles"], B["KTkeys"], B["Vkeys"]
            dma("pool", "dw0", wb, wu[u], [], ["wb"])
            if is_diff:
                for c, KTc in enumerate((KTa, KTb)):
                    mset("pool", KTc[64:65, 0:ncols], 1.0, [], [("KTone", c)] + KTkeys[c])
                    for ti, (a, n) in enumerate(ktiles):
                        b = nextbank()
                        for ch in range(8):
                            mm(bank(b)[0:64, 0:n], wb3[:, ch, 128 + 64 * c:192 + 64 * c], hT[:, ch, a:a + n],
                               ch == 0, ch == 7, ["wb", "hTall"], [PK(b)])
                        evac(KTc[0:64, a:a + n], bank(b)[0:64, 0:n], [PK(b)], [KTkeys[c][ti]])
            else:
                for ti, (a, n) in enumerate(ktiles):
                    b = nextbank()
                    for ch in range(8):
                        mm(bank(b)[:, 0:n], wb3[:, ch, 128:256], hT[:, ch, a:a + n], ch == 0, ch == 7,
                           ["wb", "hTall"], [PK(b)])
                    evac(KTa[:, a:a + n], bank(b)[:, 0:n], [PK(b)], [KTkeys[0][ti], ("KTone", 0)])
            b = nextbank()
            for ch in range(8):
                mm(bank(b)[0:16, 0:128], hT[:, ch, 0:16], wb3[:, ch, 256:384], ch == 0, ch == 7,
                   ["wb", "hTall"], [PK(b)])
            evac(V3[0:16, 0, :], bank(b)[0:16, 0:128], [PK(b)], [Vkeys[0]])
            for a4 in range(nreal // 4):
                b = nextbank()
                for j in range(4):
                    c0 = NMETA + 128 * (4 * a4 + j)
                    for ch in range(8):
                        mm(bank(b)[:, j * 128:(j + 1) * 128], hT[:, ch, c0:c0 + 128], wb3[:, ch, 256:384],
                           ch == 0, ch == 7, ["wb", "hTall"], [PK(b)])
                evac(V3[:, 1 + 4 * a4:5 + 4 * a4, :], bank(b).rearrange("p (j n) -> p j n", j=4),
                     [PK(b)], [Vkeys[1 + 4 * a4 + j] for j in range(4)])
            if is_diff:
                for c, QTc in enumerate((QTa, QTb)):
                    for sl in range(2):
                        dma("sp", "dq", QTc[64:65, sl * 512:(sl + 1) * 512], caug[h:h + 1, :], [], [("QT", c, sl)])
                    for sl in range(2):
                        b = nextbank()
                        for ch in range(8):
                            mm(bank(b)[0:64, :], wb3[:, ch, 64 * c:64 * c + 64], hqT[:, ch, sl * 512:(sl + 1) * 512],
                               ch == 0, ch == 7, ["wb", "hqTall"], [PK(b)])
                        evac(QTc[0:64, sl * 512:(sl + 1) * 512], bank(b)[0:64, :], [PK(b)], [("QT", c, sl)])
            else:
                for sl in range(2):
                    b = nextbank()
                    for ch in range(8):
                        mm(bank(b), wb3[:, ch, 0:128], hqT[:, ch, sl * 512:(sl + 1) * 512], ch == 0, ch == 7,
                           ["wb", "hqTall"], [PK(b)])
                    evac(QTa[:, sl * 512:(sl + 1) * 512], bank(b), [PK(b)], [("QT", 0, sl)])
            for sl in range(2):
                b = nextbank()
                for ch in range(8):
                    mm(bank(b), wb3[:, ch, 384:512], hqT[:, ch, sl * 512:(sl + 1) * 512], ch == 0, ch == 7,
                       ["wb", "hqTall"], [PK(b)])
                tg = Rf[8 + sl]
                act(tg, bank(b), AF.Exp, [PK(b)], RK(8 + sl), scale=-1.0)
                tsadd("dve", tg, tg, 1.0, RK(8 + sl), RK(8 + sl))
                recip(tg, tg, RK(8 + sl), RK(8 + sl))
                tt("dve", GT[:, sl * 512:(sl + 1) * 512], bank(b), tg, ALU.mult, RK(8 + sl) + [PK(b)], [("GT", sl)])

        def diff_slot(P, u, sl, B):
            h = u % 8
            KT = (B["KTa"], B["KTb"])
            QT = (B["QTa"], B["QTb"])
            V3, GT, KTkeys, Vkeys = B["V3"], B["GT"], B["KTkeys"], B["Vkeys"]
            s = 2 * P + sl
            q0 = sl * 512
            gmax = 8 * s + 7
            steps = [0, -1] + list(range(1, gmax + 1))
            nst = len(steps)
            Ob = (4, 5)
            Rb = (6, 7)
            for ti, g in enumerate(steps):
                first, last = ti == 0, ti == nst - 1
                if g < 0:
                    kr, kc0, blk, r, bidx = 16, 0, 0, -1, 32 + s
                else:
                    kr, kc0, blk, r, bidx = 128, NMETA + 128 * g, 1 + g, g - 8 * s, (g - 8 * s) + 24
                c0 = 128 * (r // 2) if r >= 0 else 0
                for c in range(2):
                    sb = (2 * ti + c) % 4
                    pbuf = Rbf(sb // 2, sb % 2)
                    pkey = [("R", sb // 2, sb % 2)]
                    mm(bank(sb)[0:kr, c0:512], KT[c][0:65, kc0:kc0 + kr], QT[c][0:65, q0 + c0:q0 + 512],
                       True, r < 0, KTkeys[c] + [("KTone", c), ("QT", c, sl)], [PK(sb)])
                    if r >= 0:
                        mm(bank(sb)[:, c0:c0 + 128], ident, masks[:, r * 128:(r + 1) * 128], False, True,
                           ["cm", "masks"], [PK(sb)])
                    act(pbuf[0:kr, c0:512], bank(sb)[0:kr, c0:512], AF.Exp, [PK(sb), "biasT"], pkey,
                        bias=biasT[0:kr, h * 36 + bidx:h * 36 + bidx + 1], scale=0.125)
                    mm(bank(Ob[c])[:, c0:512], V3[0:kr, blk, :], pbuf[0:kr, c0:512], first, last,
                       pkey + [Vkeys[blk]], [PK(Ob[c])])
                    mm(bank(Rb[c])[:, c0:512], ones[0:kr, :], pbuf[0:kr, c0:512], first, last,
                       pkey + ["cm"], [PK(Rb[c])])
            y0, y1, t2, t3 = Rf[2], Rf[3], Rf[4], Rf[5]
            for c, yy in ((0, y0), (1, y1)):
                recip(t2, bank(Rb[c]), [PK(Rb[c])], RK(4))
                tt("dve", yy, bank(Ob[c]), t2, ALU.mult, [PK(Ob[c])] + RK(4), RK(2 + c))
            stt("dve", y0, y1, neglam[:, 0:1], y0, ALU.mult, ALU.add, RK(2) + RK(3) + ["neglam"], RK(2))
            tt("pool", t3, y0, y0, ALU.mult, RK(2), RK(5))
            mm(bank(0), onesf, t3, True, True, RK(5) + ["onesf"], [PK(0)])
            tsadd("dve", t2, bank(0), RMS_EPS, [PK(0)], RK(4))
            act(t2, t2, AF.Ln, RK(4), RK(4))
            act(t2, t2, AF.Exp, RK(4), RK(4), scale=-0.5)
            tt("dve", y0, y0, t2, ALU.mult, RK(2) + RK(4), RK(2))
            stt("dve", B["YT"][u][:, q0:q0 + 512], y0, sg[:, 1:2], GT[:, q0:q0 + 512], ALU.mult, ALU.mult,
                RK(2) + ["sgs", ("GT", sl)], [("YT", u, sl, 0), ("YT", u, sl, 1)])

        def sb_head(P, u, sl, hh, B):
            KTs, QTs, V3, GT, KTkeys, Vkeys = B["KTa"], B["QTa"], B["V3"], B["GT"], B["KTkeys"], B["Vkeys"]
            s = 2 * P + sl
            q0 = sl * 512
            gmax = 8 * s + 7
            glist = list(range(gmax, -1, -1)) + [-1]
            nst = len(glist)
            ebuf, ekey = [Rf[0], Rf[1], Rf[2]], [RK(0), RK(1), RK(2)]
            xbf, xkey = [Rf[3], Rf[4]], [RK(3), RK(4)]
            S32 = Rf[5]
            spb = [Rbf(6, 0), Rbf(6, 1), Rbf(7, 0)]
            spk = [[("R", 6, 0)], [("R", 6, 1)], [("R", 7, 0)]]
            Sbf = [Rbf(7, 1), Rbf(8, 0), Rbf(8, 1)]
            Sbk = [[("R", 7, 1)], [("R", 8, 0)], [("R", 8, 1)]]
            Abf = [Rbf(9, 0), Rbf(9, 1)]
            Abk = [[("R", 9, 0)], [("R", 9, 1)]]
            p0, p1 = 64 * hh, 64 * hh + 64
            Obank = 4 + hh
            mset("pool", S32, 0.0, [], RK(5))
            mset("pool", Sbf[0], 0.0, [], Sbk[0])
            mm(bank(Obank), negones, Sbf[0], True, False, Sbk[0] + ["cm"], [PK(Obank)])

            def geom(g):
                if g < 0:
                    return 16, 0, 0, -1
                return 128, NMETA + 128 * g, 1 + g, g - 8 * s

            def stageA(ti):
                kr, kc0, blk, r = geom(glist[ti])
                c0 = 128 * (r // 2) if r >= 0 else 0
                zb = ti % 2
                e_t, ek = ebuf[ti % 3], ekey[ti % 3]
                sp_t, sk = spb[ti % 3], spk[ti % 3]
                mm(bank(zb)[0:kr, c0:512], KTs[p0:p1, kc0:kc0 + kr], QTs[p0:p1, q0 + c0:q0 + 512], True, r < 0,
                   KTkeys[0] + [("QT", 0, sl), ("KTone", 0)], [PK(zb)])
                if r >= 0:
                    mm(bank(zb)[:, c0:c0 + 128], ident, masks[:, 1024 + r * 128:1024 + (r + 1) * 128], False, True,
                       ["cm", "masks"], [PK(zb)])
                act(e_t[0:kr, c0:512], bank(zb)[0:kr, c0:512], AF.Exp, [PK(zb)], ek, scale=0.125)
                act(sp_t[0:kr, c0:512], e_t[0:kr, c0:512], AF.Ln, ek, sk, bias=1.0)
                if ti + 1 < nst:
                    tt("pool", S32[:, c0:512], S32[:, c0:512], sp_t[:, c0:512], ALU.add, RK(5) + sk, RK(5))
                    vcopy("pool", Sbf[(ti + 1) % 3], S32, RK(5), Sbk[(ti + 1) % 3])

            def stageB(ti):
                kr, kc0, blk, r = geom(glist[ti])
                c0 = 128 * (r // 2) if r >= 0 else 0
                tb = 2 + ti % 2
                e_t, ek = ebuf[ti % 3], ekey[ti % 3]
                sp_t, sk = spb[ti % 3], spk[ti % 3]
                x_t, xk = xbf[ti % 2], xkey[ti % 2]
                a_t, ak = Abf[ti % 2], Abk[ti % 2]
                first, last = ti == 0, ti == nst - 1
                mm(bank(tb)[0:kr, c0:512], negtri[0:kr, 0:kr], sp_t[0:kr, c0:512], True, False, sk + ["cm"], [PK(tb)])
                mm(bank(tb)[0:kr, c0:512], negones[:, 0:kr], Sbf[ti % 3][:, c0:512], False, True,
                   Sbk[ti % 3] + ["cm"], [PK(tb)])
                act(x_t[0:kr, c0:512], bank(tb)[0:kr, c0:512], AF.Exp, [PK(tb)], xk)
                tt("dve", a_t[0:kr, c0:512], e_t[0:kr, c0:512], x_t[0:kr, c0:512], ALU.mult, ek + xk, ak)
                mm(bank(Obank)[:, c0:512], V3[0:kr, blk, :], a_t[0:kr, c0:512], False, last,
                   ak + [Vkeys[blk]], [PK(Obank)])

            stageA(0)
            for ti in range(nst):
                if ti + 1 < nst:
                    stageA(ti + 1)
                stageB(ti)
            tt("dve", B["YT"][u][p0:p1, q0:q0 + 512], bank(Obank)[p0:p1, :], GT[p0:p1, q0:q0 + 512], ALU.mult,
               [PK(Obank), ("GT", sl)], [("YT", u, sl, hh)])

        def tail(P, B):
            KTkeys, Vkeys = B["KTkeys"], B["Vkeys"]
            A.release(mark_unit)
            wtb = A.alloc(4096, BF16)
            wm3 = wtb[:, 0:2048].rearrange("p (c n) -> p c n", c=8)
            wd3 = wtb[:, 2048:3072].rearrange("p (u n) -> p u n", u=8)
            ws3 = wtb[:, 3072:4096].rearrange("p (u n) -> p u n", u=8)
            mT = A.alloc(8 * 1024, BF16).rearrange("p (c t) -> p c t", c=8)
            wob = A.alloc(8 * 1024, BF16)
            wo3 = wob.rearrange("p (c n) -> p c n", c=8)
            ytall = [("YT", u, sl, hh) for u in range(16) for sl in range(2) for hh in range(2)]
            free_keys = ytall + ["wb", ("KTone", 0), ("KTone", 1)] + KTkeys[0] + KTkeys[1] + Vkeys + \
                [("QT", c, sl) for c in range(2) for sl in range(2)] + [("GT", 0), ("GT", 1)]
            mset("dve", dummy[:, 4:5], 0.0, free_keys, ["YTall", "tailfree"] + free_keys)
            mset("pool", dummy[:, 5:6], 0.0, ["tailfree"], ["tailfree2"])
            dma("pool", "dw2", wob, wo, ["tailfree2"], ["wob"])
            for cc in range(8):
                dma("pool", "dw1", wtb, wt[cc], ["tailfree2"], ["wtb"])
                for sl in range(2):
                    q0 = sl * 512
                    bd, bs, bmd, bms = 0, 1, 2, 3
                    for uu in range(8):
                        mm(bank(bd), wd3[:, uu, :], YT[uu][:, q0:q0 + 512], uu == 0, uu == 7, ["wtb", "YTall"], [PK(bd)])
                    for uu in range(8):
                        mm(bank(bs), ws3[:, uu, :], YT[8 + uu][:, q0:q0 + 512], uu == 0, uu == 7,
                           ["wtb", "YTall"], [PK(bs)])
                    for ch in range(8):
                        mm(bank(bmd), wm3[:, ch, 0:128], hqT[:, ch, q0:q0 + 512], ch == 0, ch == 7,
                           ["wtb", "hqTall"], [PK(bmd)])
                    for ch in range(8):
                        mm(bank(bms), wm3[:, ch, 128:256], hqT[:, ch, q0:q0 + 512], ch == 0, ch == 7,
                           ["wtb", "hqTall"], [PK(bms)])
                    sd, ss = Rf[6], Rf[7]
                    for (bm, sgt, rk, br) in ((bmd, sd, 6, 0), (bms, ss, 7, 1)):
                        act(sgt, bank(bm), AF.Exp, [PK(bm), "bg"], RK(rk),
                            bias=bg[:, br * 8 + cc:br * 8 + cc + 1], scale=-1.0)
                        tsadd("dve", sgt, sgt, 1.0, RK(rk), RK(rk))
                        recip(sgt, sgt, RK(rk), RK(rk))
                    tt("dve", sd, bank(bd), sd, ALU.mult, [PK(bd)] + RK(6), RK(6))
                    tt("dve", ss, bank(bs), ss, ALU.mult, [PK(bs)] + RK(7), RK(7))
                    tt("pool", mT[:, cc, q0:q0 + 512], sd, ss, ALU.add, RK(6) + RK(7), [("mT", cc, sl)])
            mkeys = [("mT", cc, sl) for cc in range(8) for sl in range(2)]
            for jb in range(8):
                i = jb % 2
                r0 = (8 * P + jb) * 128
                xa, xb2 = xbuf[i]
                load_x(i, xq[r0:r0 + 128, :], 128)
                ln_core(i, 128, 0, (xa, xb2), (RK(2 * i), RK(2 * i + 1)))
                for hf in range(2):
                    b = 4 + 2 * i + hf
                    for cc in range(8):
                        mm(bank(b), mT[:, cc, jb * 128:(jb + 1) * 128], wo3[:, cc, hf * 512:(hf + 1) * 512],
                           cc == 0, cc == 7, mkeys + ["wob"], [PK(b)])
                    xs = (xa, xb2)[hf]
                    stt("dve", xs, xs, DN_ALPHA, bank(b), ALU.mult, ALU.add, RK(2 * i + hf) + [PK(b)], RK(2 * i + hf))
                ln_core(i, 128, 2, (xa, xb2), (RK(2 * i), RK(2 * i + 1)))
                dma("sp", "do%d" % i, out[r0:r0 + 128, 0:512], xa, RK(2 * i), [])
                dma("sp", "do%d" % i, out[r0:r0 + 128, 512:1024], xb2, RK(2 * i + 1), [])
            tk = mkeys + ["wob", "wtb", "YTall", "hqTall"]
            mset("dve", dummy[:, 6:7], 0.0, tk, ["tailend"] + tk)

        for P in passes:
            nreal = 16 * (P + 1)
            ncols = NMETA + 128 * nreal
            A.release(mark_unit)
            B = {"nreal": nreal, "ncols": ncols, "YT": YT}
            B["wb"] = A.alloc(8 * 512, BF16)
            B["wb3"] = B["wb"].rearrange("p (c n) -> p c n", c=8)
            B["KTa"] = A.alloc(L, BF16)
            B["KTb"] = A.alloc(L, BF16)
            B["V3"] = A.alloc(33 * 128, BF16).rearrange("p (b n) -> p b n", b=33)
            B["QTa"] = A.alloc(1024, BF16)
            B["QTb"] = A.alloc(1024, BF16)
            B["GT"] = A.alloc(1024, F32)
            B["ktiles"] = [(a, min(512, ncols - a)) for a in range(0, ncols, 512)]
            B["KTkeys"] = [[("KT", c, a) for (a, n) in B["ktiles"]] for c in range(2)]
            B["Vkeys"] = [("V", b) for b in range(1 + nreal)]
            if P != passes[0]:
                allk = ["wb", ("KTone", 0), ("KTone", 1)] + B["KTkeys"][0] + B["KTkeys"][1] + B["Vkeys"] + \
                    [("QT", c, sl) for c in range(2) for sl in range(2)] + [("GT", 0), ("GT", 1)]
                allk += [("YT", u_, sl_, hh_) for u_ in range(16) for sl_ in range(2) for hh_ in range(2)]
                allk += [("hqT", k_) for k_ in range(8)]
                mset("dve", dummy[:, 7:8], 0.0, ["tailend"], allk)

            for jb in range(8):
                r0 = (8 * P + jb) * 128
                ln_block_T(xq[r0:r0 + 128, :], 128, hqT, jb * 128, [("hqT", jb), "hqTall"])
            join([("hqT", k) for k in range(8)], "hqTall", 2)

            for u in units:
                unit_proj(P, u, B)
                for sl in range(2):
                    if u < 8:
                        diff_slot(P, u, sl, B)
                    else:
                        sb_head(P, u, sl, 0, B)
                        sb_head(P, u, sl, 1, B)

            if stage == "U":
                if P == passes[0]:
                    for k, u in enumerate(units[:2]):
                        for hf in range(2):
                            tmp = Rf[2 * k + hf]
                            vcopy("dve", tmp, YT[u][:, hf * 512:(hf + 1) * 512],
                                  [("YT", u, hf, 0), ("YT", u, hf, 1)], RK(2 * k + hf))
                            dma("sp", "do0", dbg[:, k * 1024 + hf * 512:k * 1024 + hf * 512 + 512], tmp,
                                RK(2 * k + hf), [])
                continue
            tail(P, B)

        T.emit(nc, sems, block)
    return nc


def _bf16(a):
    return np.asarray(a, dtype=np.float32).astype(ml_dtypes.bfloat16)


def _const_tables(p):
    k = np.arange(128)[:, None]
    i = np.arange(128)[None, :]
    diagD = np.where(k > i, NEG, 0.0).astype(np.float32)
    diagS = np.where(k >= i, NEG, 0.0).astype(np.float32)
    full = np.full((128, 128), NEG, np.float32)
    zero = np.zeros((128, 128), np.float32)
    m = np.zeros((128, 2, 8, 128), np.float32)
    for r in range(8):
        for bi, dg in enumerate((diagD, diagS)):
            if p == 0:
                m[:, bi, r, :] = dg if r % 2 == 0 else full
            else:
                m[:, bi, r, :] = zero if r % 2 == 0 else dg
    cmask = _bf16(m.reshape(128, 2048))
    bias = np.zeros((128, 8, 36), np.float32)
    kk = np.arange(128, dtype=np.float64)
    for h in range(8):
        for idx in range(32):
            delta = idx - 24
            bias[:, h, idx] = SLOPES[h] * (128.0 * (delta - p) + kk)
        for s in range(4):
            bias[:, h, 32 + s] = SLOPES[h] * (kk - 16.0 - 128.0 * (8 * s + p))
    cbias = bias.reshape(128, 8 * 36)
    col = np.arange(512)
    pos = 256.0 * (col // 128) + (col % 128)
    caug = _bf16(np.stack([-8.0 * SLOPES[h] * pos for h in range(8)]))
    j = np.arange(128)[:, None]
    kc = np.arange(128)[None, :]
    ident = (j == kc).astype(np.float32)
    negtri = -(j >= kc).astype(np.float32)
    negones = -np.ones((128, 128), np.float32)
    ones = np.ones((128, 128), np.float32)
    cmat = _bf16(np.concatenate([ident, negtri, negones, ones], axis=1))
    return cmask, cbias, caug, cmat


def _pc(w):
    n = w.shape[1]
    return np.ascontiguousarray(w.reshape(8, 128, n).transpose(1, 0, 2).reshape(128, 8 * n))


def make_in_maps(inputs):
    x = np.asarray(inputs["x"], np.float32)
    w_in = np.asarray(inputs["w_in"], np.float32)[0]
    secs = np.split(w_in, 10, axis=1)
    units = []
    for u in range(16):
        base = 0 if u < 8 else 4
        j = u % 8
        cols = [secs[base + t][:, j * 128:(j + 1) * 128] for t in range(4)]
        units.append(_pc(np.concatenate(cols, axis=1)))
    wu = np.stack(units)
    wbd = np.asarray(inputs["w_br_diff"], np.float32)[0]
    wbs = np.asarray(inputs["w_br_sb"], np.float32)[0]
    wt = np.zeros((8, 128, 4096), np.float32)
    for cc in range(8):
        cs = slice(cc * 128, (cc + 1) * 128)
        mcols = np.concatenate([secs[8][:, cs], secs[9][:, cs]], axis=1)
        wt[cc, :, 0:2048] = _pc(mcols)
        wt[cc, :, 2048:3072] = _pc(wbd[:, cs])
        wt[cc, :, 3072:4096] = _pc(wbs[:, cs])
    wo = _pc(np.asarray(inputs["w_out"], np.float32)[0])
    lnv = np.stack([np.asarray(inputs["emb_ln_g"], np.float32), np.asarray(inputs["emb_ln_b"], np.float32),
                    np.asarray(inputs["ln_g"], np.float32)[0], np.asarray(inputs["ln_b"], np.float32)[0]])
    bgate = np.ascontiguousarray(
        np.asarray(inputs["b_gate"], np.float32)[0].reshape(2, 8, 128).transpose(2, 0, 1).reshape(128, 16))
    lamv = np.asarray(inputs["diff_lambda"], np.float32)[0].reshape(1, 256)
    subg = np.asarray(inputs["diff_subln_g"], np.float32)[0].reshape(128, 1)
    meta = np.asarray(inputs["meta_tokens"], np.float32)
    maps = []
    for c in range(NCORES):
        b, p = c // 2, c % 2
        cmask, cbias, caug, cmat = _const_tables(p)
        xq = np.ascontiguousarray(x[b].reshape(16, 2, 128, D)[:, p].reshape(SEQ // 2, D))
        maps.append({"xf": np.ascontiguousarray(x[b]), "xq": xq, "meta": meta, "lnv": lnv, "wu": wu,
                     "wt": wt, "wo": wo, "bgate": bgate, "lamv": lamv, "subg": subg,
                     "cmask": cmask, "cbias": cbias, "caug": caug, "cmat": cmat})
    return maps


def kernel(**inputs):
    nc = build_program("full")
    maps = make_in_maps(inputs)
    res = run_bass_kernel_spmd(nc, maps, core_ids=list(range(NCORES)))
    outp = np.zeros((4, SEQ, D), np.float32)
    for c in range(NCORES):
        b, p = c // 2, c % 2
        outp[b].reshape(16, 2, 128, D)[:, p] = np.asarray(res.results[c]["out"]).reshape(16, 128, D)
    return outp
```

```python
import numpy as np
import ml_dtypes
import concourse.bass as bass
import concourse.mybir as mybir
from concourse.bass_utils import run_bass_kernel_spmd

F32 = mybir.dt.float32
BF16 = mybir.dt.bfloat16
AF = mybir.ActivationFunctionType
ALU = mybir.AluOpType

D = 1024
SEQ = 4096
NMETA = 16
L = SEQ + NMETA
NCORES = 8
NEG = -30000.0
LN_EPS = 1e-5
RMS_EPS = 1e-5
DN_ALPHA = 2.0 ** 0.25
LAM_INIT = 0.8 - 0.6 * 1.0
SLOPES = [2.0 ** (-(h + 1)) for h in range(8)]


class _Op:
    __slots__ = ("fn", "eng", "lane", "lidx", "deps", "dma", "signal")


class Tracker:
    COMPUTE = ("pe", "act", "dve", "pool")

    def __init__(self):
        self.streams = {e: [] for e in ("pe", "act", "dve", "pool", "sp")}
        self.lane_ops = {}
        self.last_w = {}
        self.readers = {}

    def _add(self, eng, lane, fn, reads, writes, dma):
        op = _Op()
        op.fn, op.eng, op.lane, op.dma, op.signal = fn, eng, lane, dma, dma
        lst = self.lane_ops.setdefault(lane, [])
        lst.append(op)
        op.lidx = len(lst)
        deps = {}

        def add(d, raw=False):
            if d is None:
                return
            if d.lane == lane and not dma and lane == "pe":
                return
            cur = deps.get(d.lane)
            if cur is None or cur.lidx < d.lidx:
                deps[d.lane] = d

        for k in reads:
            add(self.last_w.get(k), True)
        for k in writes:
            add(self.last_w.get(k))
            for r in self.readers.get(k, {}).values():
                add(r)
        if dma and len(lst) > 1:
            add(lst[-2])
        op.deps = list(deps.values())
        for k in reads:
            self.readers.setdefault(k, {})[lane] = op
        for k in writes:
            self.last_w[k] = op
            self.readers[k] = {}
        self.streams[eng].append(op)
        return op

    def op(self, eng, fn, reads=(), writes=()):
        return self._add(eng, eng, fn, reads, writes, False)

    def dma(self, eng, lane, fn, reads=(), writes=()):
        return self._add(eng, lane, fn, reads, writes, True)

    def emit(self, nc, sems, block):
        for e, ops in self.streams.items():
            for op in ops:
                for d in op.deps:
                    d.signal = True
        val = {}
        for lane, ops in self.lane_ops.items():
            c = 0
            for op in ops:
                if op.signal:
                    c += 16 if op.dma else 1
                val[(lane, op.lidx)] = c
        final = {lane: val[(lane, len(ops))] for lane, ops in self.lane_ops.items()
                 if ops and ops[-1].dma}
        streams = self.streams

        def run(engname, eng):
            waited = {}
            for op in streams[engname]:
                for d in op.deps:
                    v = val[(d.lane, d.lidx)]
                    if waited.get(d.lane, 0) < v:
                        eng.wait_ge(sems[d.lane], v)
                        waited[d.lane] = v
                ins = op.fn(eng)
                if op.signal:
                    ins.then_inc(sems[op.lane], 16 if op.dma else 1)
            if engname == "sp":
                for lane, v in final.items():
                    eng.wait_ge(sems[lane], v)

        @block.tensor
        def _(e):
            run("pe", e)

        @block.scalar
        def _(e):
            run("act", e)

        @block.vector
        def _(e):
            run("dve", e)

        @block.gpsimd
        def _(e):
            run("pool", e)

        @block.sync
        def _(e):
            run("sp", e)


class Arena:
    def __init__(self, base_ap, nwords):
        self.base = base_ap
        self.n = nwords
        self.top = 0

    def mark(self):
        return self.top

    def release(self, m):
        self.top = m

    def alloc(self, nelem, dtype=F32, parts=128):
        if dtype == F32:
            w = nelem
        else:
            w = (nelem + 1) // 2
        w = (w + 7) // 8 * 8
        a = self.top
        self.top += w
        assert self.top <= self.n, f"arena overflow {self.top} > {self.n}"
        ap = self.base[0:parts, a:a + w]
        if dtype != F32:
            ap = ap.bitcast(dtype)
        return ap[:, 0:nelem]


def build_program(stage="full", units=None, passes=(0, 1)):
    nc = bass.Bass("TRN2", target_bir_lowering=False)
    T = Tracker()
    if units is None:
        units = list(range(16))
    AX = mybir.AxisListType.X

    def din(name, shape, dt=F32):
        return nc.dram_tensor(name, list(shape), dt, kind="ExternalInput").ap()

    xf = din("xf", [SEQ, D])
    xq = din("xq", [SEQ // 2, D])
    meta = din("meta", [NMETA, D])
    lnv = din("lnv", [4, D])
    wu = din("wu", [16, 128, 8 * 512])
    wt = din("wt", [8, 128, 4096])
    wo = din("wo", [128, 8 * 1024])
    bgate = din("bgate", [128, 16])
    lamv = din("lamv", [1, 256])
    subg = din("subg", [128, 1])
    cmask = din("cmask", [128, 2 * 8 * 128], BF16)
    cbias = din("cbias", [128, 8 * 36])
    caug = din("caug", [8, 512], BF16)
    cmat = din("cmat", [128, 4 * 128], BF16)
    out = nc.dram_tensor("out", [SEQ // 2, D], F32, kind="ExternalOutput").ap()
    dbg = None
    if stage != "full":
        dbg = nc.dram_tensor("dbg", [128, 8192], F32, kind="ExternalOutput").ap()

    NW = 53000
    lanes = ["pe", "act", "dve", "pool", "dx0", "dx1", "dw0", "dw1", "dw2", "dc", "do0", "do1", "dq"]
    import contextlib
    with contextlib.ExitStack() as es:
        arena_t = es.enter_context(nc.sbuf_tensor("arena", [128, NW], F32))
        ps_t = es.enter_context(nc.psum_tensor("ps", [128, 8 * 512], F32))
        sems = {ln: es.enter_context(nc.semaphore("s_" + ln)) for ln in lanes}
        block = es.enter_context(nc.Block())
        A = Arena(arena_t[:], NW)
        PS = ps_t[:]

        def bank(b):
            return PS[:, b * 512:(b + 1) * 512]

        def PK(b):
            return ("ps", b)

        def mm(out_, lhsT, rhs, start, stop, reads, writes):
            T.op("pe", lambda e: e.matmul(out=out_, lhsT=lhsT, rhs=rhs, start=start, stop=stop),
                 reads=reads, writes=writes)

        def tr(out_, in_, idn, reads, writes):
            T.op("pe", lambda e: e.transpose(out=out_, in_=in_, identity=idn), reads=reads, writes=writes)

        def act(out_, in_, func, reads, writes, bias=None, scale=None):
            kw = {}
            if bias is not None:
                kw["bias"] = bias
            if scale is not None:
                kw["scale"] = scale
            T.op("act", lambda e: e.activation(out=out_, in_=in_, func=func, **kw), reads=reads, writes=writes)

        def acopy(out_, in_, reads, writes):
            T.op("act", lambda e: e.copy(out=out_, in_=in_), reads=reads, writes=writes)

        def vcopy(eng, out_, in_, reads, writes):
            T.op(eng, lambda e: e.tensor_copy(out=out_, in_=in_), reads=reads, writes=writes)

        def tt(eng, out_, in0, in1, op, reads, writes):
            T.op(eng, lambda e: e.tensor_tensor(out=out_, in0=in0, in1=in1, op=op), reads=reads, writes=writes)

        def ts(eng, out_, in0, s1, s2, op0, op1, reads, writes):
            T.op(eng, lambda e: e.tensor_scalar(out=out_, in0=in0, scalar1=s1, scalar2=s2, op0=op0, op1=op1),
                 reads=reads, writes=writes)

        def tsadd(eng, out_, in0, s1, reads, writes):
            T.op(eng, lambda e: e.tensor_scalar_add(out=out_, in0=in0, scalar1=s1), reads=reads, writes=writes)

        def tsmul(eng, out_, in0, s1, reads, writes):
            T.op(eng, lambda e: e.tensor_scalar_mul(out=out_, in0=in0, scalar1=s1), reads=reads, writes=writes)

        def stt(eng, out_, in0, scalar, in1, op0, op1, reads, writes):
            T.op(eng, lambda e: e.scalar_tensor_tensor(out=out_, in0=in0, scalar=scalar, in1=in1, op0=op0, op1=op1),
                 reads=reads, writes=writes)

        def recip(out_, in_, reads, writes):
            T.op("dve", lambda e: e.reciprocal(out=out_, in_=in_), reads=reads, writes=writes)

        def mset(eng, ap, val, reads, writes):
            T.op(eng, lambda e: e.memset(ap, val), reads=reads, writes=writes)

        def dma(eng, lane, out_, in_, reads, writes):
            if eng == "pool":
                T.dma(eng, lane, lambda e: e.dma_start(out=out_, in_=in_, max_dma_last_dim=4096),
                      reads=reads, writes=writes)
            else:
                T.dma(eng, lane, lambda e: e.dma_start(out=out_, in_=in_), reads=reads, writes=writes)

        pbctr = [0]

        def nextbank():
            pbctr[0] = (pbctr[0] + 1) % 8
            return pbctr[0]

        evctr = [0]

        def evac(out_ap, in_ap, reads, writes):
            evctr[0] += 1
            if evctr[0] % 2:
                acopy(out_ap, in_ap, reads, writes)
            else:
                vcopy("dve", out_ap, in_ap, reads, writes)

        cm = A.alloc(512, BF16)
        ident, negtri, negones, ones = (cm[:, i * 128:(i + 1) * 128] for i in range(4))
        masks = A.alloc(2048, BF16)
        biasT = A.alloc(8 * 36, F32)
        gb = A.alloc(4 * D, F32)
        bg = A.alloc(16, F32)
        sg = A.alloc(8, F32)
        onesf = A.alloc(128, F32)
        lamt = A.alloc(640, F32)
        neglam = A.alloc(8, F32)
        dummy = A.alloc(8, F32)
        dma("sp", "dc", cm, cmat, [], ["cm"])
        dma("sp", "dc", masks, cmask, [], ["masks"])
        dma("sp", "dc", biasT, cbias, [], ["biasT"])
        dma("sp", "dc", bg, bgate, [], ["bg"])
        dma("sp", "dc", sg[:, 0:1], subg, [], ["sg"])
        l0 = lamt[0:1, :]
        dma("sp", "dc", l0[:, 0:256], lamv, [], ["lamt"])
        mset("pool", onesf, 1.0 / 128.0, [], ["onesf"])
        tsmul("dve", sg[:, 1:2], sg[:, 0:1], 1.0 - LAM_INIT, ["sg"], ["sgs"])
        tsmul("dve", bg, bg, -1.0, ["bg"], ["bg"])
        tt("dve", l0[:, 256:320], l0[:, 0:64], l0[:, 64:128], ALU.mult, ["lamt"], ["lam1"])
        tt("dve", l0[:, 320:384], l0[:, 128:192], l0[:, 192:256], ALU.mult, ["lamt"], ["lam2"])
        T.op("dve", lambda e: e.reduce_sum(out=l0[:, 384:385], in_=l0[:, 256:320], axis=AX),
             reads=["lam1"], writes=["lam3a"])
        T.op("dve", lambda e: e.reduce_sum(out=l0[:, 385:386], in_=l0[:, 320:384], axis=AX),
             reads=["lam2"], writes=["lam3b"])
        act(l0[:, 386:388], l0[:, 384:386], AF.Exp, ["lam3a", "lam3b"], ["lam4"])
        stt("dve", l0[:, 388:389], l0[:, 387:388], -LAM_INIT, l0[:, 386:387], ALU.add, ALU.subtract,
            ["lam4"], ["lam5"])
        mset("pool", l0[:, 400:528], 1.0, [], ["one1"])
        mm(bank(0)[:, 0:1], l0[:, 400:528], l0[:, 388:389], True, True, ["one1", "lam5"], [PK(0)])
        vcopy("dve", neglam[:, 0:1], bank(0)[:, 0:1], [PK(0)], ["neglam"])

        hT = A.alloc(8 * L, BF16).rearrange("p (c t) -> p c t", c=8)
        NR = 10
        Rf = [A.alloc(512, F32) for _ in range(NR)]

        Rf = Rf + [gb[:, j * 512:(j + 1) * 512] for j in range(8)]

        def RK(i):
            if i >= 10:
                return [("gbR", i - 10, 0), ("gbR", i - 10, 1)]
            return [("R", i, 0), ("R", i, 1)]

        def RKh(i, half):
            return [RK(i)[half]]

        def gbk(i):
            return [("gb", i)] + RK(10 + 2 * i) + RK(11 + 2 * i)

        def load_gb():
            for i in range(4):
                dma("sp", "dc", gb[:, i * D:(i + 1) * D], lnv[i:i + 1, :].partition_broadcast(128), [], gbk(i))

        def Rbf(i, half):
            return Rf[i].bitcast(BF16)[:, half * 512:(half + 1) * 512]

        xbuf = [(Rf[0], Rf[1]), (Rf[2], Rf[3])]
        load_gb()
        stat = [A.alloc(16, F32), A.alloc(16, F32)]
        lncount = [0]

        def ln_core(i, rows, gi, out_tiles, out_keys):
            xs = (xbuf[i][0][0:rows], xbuf[i][1][0:rows])
            xk = RK(2 * i) + RK(2 * i + 1)
            st = stat[i][0:rows]
            sk = ("st", i)
            for hf in range(2):
                xh = xs[hf]
                so = st[:, 6 * hf:6 * hf + 6]
                T.op("dve", (lambda xh, so: (lambda e: e.bn_stats(out=so, in_=xh)))(xh, so), reads=xk, writes=[sk])
            T.op("dve", lambda e: e.bn_aggr(out=st[:, 12:14], in_=st[:, 0:12]), reads=[sk], writes=[sk])
            tsadd("dve", st[:, 15:16], st[:, 13:14], LN_EPS, [sk], [sk])
            act(st[:, 15:16], st[:, 15:16], AF.Ln, [sk], [sk])
            act(st[:, 14:15], st[:, 15:16], AF.Exp, [sk], [sk], scale=-0.5)
            for hf in range(2):
                hk = RK(2 * i + hf)
                ts("dve", xs[hf], xs[hf], st[:, 12:13], st[:, 14:15], ALU.subtract, ALU.mult, xk + [sk], hk)
                tt("pool", xs[hf], xs[hf], gb[0:rows, gi * D + hf * 512: gi * D + hf * 512 + 512], ALU.mult,
                   hk + gbk(gi), hk)
                tt("dve", out_tiles[hf], xs[hf],
                   gb[0:rows, (gi + 1) * D + hf * 512:(gi + 1) * D + hf * 512 + 512], ALU.add,
                   hk + gbk(gi + 1), out_keys[hf])

        def load_x(i, src_ap, rows):
            dma("sp", "dx%d" % i, xbuf[i][0][0:rows], src_ap[:, 0:512], [], RK(2 * i))
            dma("sp", "dx%d" % i, xbuf[i][1][0:rows], src_ap[:, 512:1024], [], RK(2 * i + 1))

        def ln_block_T(src_ap, rows, dstT, col0, key):
            i = lncount[0] % 2
            lncount[0] += 1
            load_x(i, src_ap, rows)
            hb = Rf[4 + i].bitcast(BF16)[0:rows]
            ln_core(i, rows, 0, (hb[:, 0:512], hb[:, 512:1024]), ([("R", 4 + i, 0)], [("R", 4 + i, 1)]))
            pb = nextbank()
            pst = bank(pb).bitcast(BF16).rearrange("p (c t) -> p c t", c=8)
            for c in range(8):
                tr(pst[:, c, 0:rows], hb[:, c * 128:(c + 1) * 128], ident[0:rows, 0:rows],
                   RK(4 + i) + ["cm"], [PK(pb)])
            evac(dstT[:, :, col0:col0 + rows], pst[:, :, 0:rows], [PK(pb)], key if isinstance(key, list) else [key])

        def join(keys, name, slot):
            mset("dve", dummy[:, slot:slot + 1], 0.0, keys, [name + "_d"])
            acopy(dummy[:, slot + 1:slot + 2], dummy[:, slot:slot + 1], keys + [name + "_d"], [name])

        ln_block_T(meta, NMETA, hT, 0, ("hT", 0))
        for g in range(32):
            ln_block_T(xf[g * 128:(g + 1) * 128, :], 128, hT, NMETA + g * 128, ("hT", 1 + g))
        join([("hT", k) for k in range(33)], "hTall", 0)

        hqT = A.alloc(8 * 1024, BF16).rearrange("p (c t) -> p c t", c=8)
        YT = [A.alloc(1024, BF16) for _ in range(16)]
        mark_unit = A.mark()

        def unit_proj(P, u, B):
            is_diff = u < 8
            h = u % 8
            wb, wb3 = B["wb"], B["wb3"]
            KTa, KTb, V3, QTa, QTb, GT = B["KTa"], B["KTb"], B["V3"], B["QTa"], B["QTb"], B["GT"]
            nreal, ncols, ktiles, KTkeys, Vkeys = B["nreal"], B["ncols"], B["ktiles"], B["KTkeys"], B["Vkeys"]
            dma("pool", "dw0", wb, wu[u], [], ["wb"])
            if is_diff:
                for c, KTc in enumerate((KTa, KTb)):
                    mset("pool", KTc[64:65, 0:ncols], 1.0, [], [("KTone", c)] + KTkeys[c])
                    for ti, (a, n) in enumerate(ktiles):
                        b = nextbank()
                        for ch in range(8):
                            mm(bank(b)[0:64, 0:n], wb3[:, ch, 128 + 64 * c:192 + 64 * c], hT[:, ch, a:a + n],
                               ch == 0, ch == 7, ["wb", "hTall"], [PK(b)])
                        evac(KTc[0:64, a:a + n], bank(b)[0:64, 0:n], [PK(b)], [KTkeys[c][ti]])
            else:
                for ti, (a, n) in enumerate(ktiles):
                    b = nextbank()
                    for ch in range(8):
                        mm(bank(b)[:, 0:n], wb3[:, ch, 128:256], hT[:, ch, a:a + n], ch == 0, ch == 7,
                           ["wb", "hTall"], [PK(b)])
                    evac(KTa[:, a:a + n], bank(b)[:, 0:n], [PK(b)], [KTkeys[0][ti], ("KTone", 0)])
            b = nextbank()
            for ch in range(8):
                mm(bank(b)[0:16, 0:128], hT[:, ch, 0:16], wb3[:, ch, 256:384], ch == 0, ch == 7,
                   ["wb", "hTall"], [PK(b)])
            evac(V3[0:16, 0, :], bank(b)[0:16, 0:128], [PK(b)], [Vkeys[0]])
            for a4 in range(nreal // 4):
                b = nextbank()
                for j in range(4):
                    c0 = NMETA + 128 * (4 * a4 + j)
                    for ch in range(8):
                        mm(bank(b)[:, j * 128:(j + 1) * 128], hT[:, ch, c0:c0 + 128], wb3[:, ch, 256:384],
                           ch == 0, ch == 7, ["wb", "hTall"], [PK(b)])
                evac(V3[:, 1 + 4 * a4:5 + 4 * a4, :], bank(b).rearrange("p (j n) -> p j n", j=4),
                     [PK(b)], [Vkeys[1 + 4 * a4 + j] for j in range(4)])
            if is_diff:
                for c, QTc in enumerate((QTa, QTb)):
                    for sl in range(2):
                        dma("sp", "dq", QTc[64:65, sl * 512:(sl + 1) * 512], caug[h:h + 1, :], [], [("QT", c, sl)])
                    for sl in range(2):
                        b = nextbank()
                        for ch in range(8):
                            mm(bank(b)[0:64, :], wb3[:, ch, 64 * c:64 * c + 64], hqT[:, ch, sl * 512:(sl + 1) * 512],
                               ch == 0, ch == 7, ["wb", "hqTall"], [PK(b)])
                        evac(QTc[0:64, sl * 512:(sl + 1) * 512], bank(b)[0:64, :], [PK(b)], [("QT", c, sl)])
            else:
                for sl in range(2):
                    b = nextbank()
                    for ch in range(8):
                        mm(bank(b), wb3[:, ch, 0:128], hqT[:, ch, sl * 512:(sl + 1) * 512], ch == 0, ch == 7,
                           ["wb", "hqTall"], [PK(b)])
                    evac(QTa[:, sl * 512:(sl + 1) * 512], bank(b), [PK(b)], [("QT", 0, sl)])
            for sl in range(2):
                b = nextbank()
                for ch in range(8):
                    mm(bank(b), wb3[:, ch, 384:512], hqT[:, ch, sl * 512:(sl + 1) * 512], ch == 0, ch == 7,
                       ["wb", "hqTall"], [PK(b)])
                tg = Rf[8 + sl]
                act(tg, bank(b), AF.Exp, [PK(b)], RK(8 + sl), scale=-1.0)
                tsadd("dve", tg, tg, 1.0, RK(8 + sl), RK(8 + sl))
                recip(tg, tg, RK(8 + sl), RK(8 + sl))
                tt("dve", GT[:, sl * 512:(sl + 1) * 512], bank(b), tg, ALU.mult, RK(8 + sl) + [PK(b)], [("GT", sl)])

        def diff_slot(P, u, sl, B):
            h = u % 8
            KT = (B["KTa"], B["KTb"])
            QT = (B["QTa"], B["QTb"])
            V3, GT, KTkeys, Vkeys = B["V3"], B["GT"], B["KTkeys"], B["Vkeys"]
            s = 2 * P + sl
            q0 = sl * 512
            gmax = 8 * s + 7
            steps = [0, -1] + list(range(1, gmax + 1))
            nst = len(steps)
            Ob = (4, 5)
            Rb = (6, 7)
            items = []
            for ti, g in enumerate(steps):
                first, last = ti == 0, ti == nst - 1
                if g < 0:
                    kr, kc0, blk, r, bidx = 16, 0, 0, -1, 32 + s
                else:
                    kr, kc0, blk, r, bidx = 128, NMETA + 128 * g, 1 + g, g - 8 * s, (g - 8 * s) + 24
                c0 = 128 * (r // 2) if r >= 0 else 0
                for c in range(2):
                    items.append((c, first, last, kr, kc0, blk, r, bidx, c0))

            def qk(n):
                c, first, last, kr, kc0, blk, r, bidx, c0 = items[n]
                sb = n % 4
                mm(bank(sb)[0:kr, c0:512], KT[c][0:65, kc0:kc0 + kr], QT[c][0:65, q0 + c0:q0 + 512],
                   True, r < 0, KTkeys[c] + [("KTone", c), ("QT", c, sl)], [PK(sb)])
                if r >= 0:
                    mm(bank(sb)[:, c0:c0 + 128], ident, masks[:, r * 128:(r + 1) * 128], False, True,
                       ["cm", "masks"], [PK(sb)])

            def rest(n):
                c, first, last, kr, kc0, blk, r, bidx, c0 = items[n]
                sb = n % 4
                pbuf = Rbf(sb // 2, sb % 2)
                pkey = [("R", sb // 2, sb % 2)]
                act(pbuf[0:kr, c0:512], bank(sb)[0:kr, c0:512], AF.Exp, [PK(sb), "biasT"], pkey,
                    bias=biasT[0:kr, h * 36 + bidx:h * 36 + bidx + 1], scale=0.125)
                mm(bank(Ob[c])[:, c0:512], V3[0:kr, blk, :], pbuf[0:kr, c0:512], first, last,
                   pkey + [Vkeys[blk]], [PK(Ob[c])])
                mm(bank(Rb[c])[:, c0:512], ones[0:kr, :], pbuf[0:kr, c0:512], first, last,
                   pkey + ["cm"], [PK(Rb[c])])

            nit = len(items)
            qk(0)
            qk(1)
            for n in range(nit):
                if n + 2 < nit:
                    qk(n + 2)
                rest(n)
            y0, y1, t2, t3 = Rf[2], Rf[3], Rf[4], Rf[5]
            for c, yy in ((0, y0), (1, y1)):
                recip(t2, bank(Rb[c]), [PK(Rb[c])], RK(4))
                tt("dve", yy, bank(Ob[c]), t2, ALU.mult, [PK(Ob[c])] + RK(4), RK(2 + c))
            stt("dve", y0, y1, neglam[:, 0:1], y0, ALU.mult, ALU.add, RK(2) + RK(3) + ["neglam"], RK(2))
            tt("pool", t3, y0, y0, ALU.mult, RK(2), RK(5))
            mm(bank(0), onesf, t3, True, True, RK(5) + ["onesf"], [PK(0)])
            tsadd("dve", t2, bank(0), RMS_EPS, [PK(0)], RK(4))
            act(t2, t2, AF.Ln, RK(4), RK(4))
            act(t2, t2, AF.Exp, RK(4), RK(4), scale=-0.5)
            tt("dve", y0, y0, t2, ALU.mult, RK(2) + RK(4), RK(2))
            stt("dve", B["YT"][u][:, q0:q0 + 512], y0, sg[:, 1:2], GT[:, q0:q0 + 512], ALU.mult, ALU.mult,
                RK(2) + ["sgs", ("GT", sl)], [("YT", u, sl, 0), ("YT", u, sl, 1)])

        def sb_slot(P, u, sl, B):
            KTs, QTs, V3, GT, KTkeys, Vkeys = B["KTa"], B["QTa"], B["V3"], B["GT"], B["KTkeys"], B["Vkeys"]
            s = 2 * P + sl
            q0 = sl * 512
            gmax = 8 * s + 7
            glist = list(range(gmax, -1, -1)) + [-1]
            nst = len(glist)

            class St:
                pass
            sts = []
            for hh in range(2):
                o = St()
                tb = 9 * hh
                o.hh = hh
                o.p0, o.p1 = 64 * hh, 64 * hh + 64
                o.e, o.ek = [Rf[tb], Rf[tb + 1]], [RK(tb), RK(tb + 1)]
                o.x, o.xk = [Rf[tb + 2], Rf[tb + 3]], [RK(tb + 2), RK(tb + 3)]
                o.S, o.Sk = Rf[tb + 4], RK(tb + 4)
                o.sp, o.spk = [Rbf(tb + 5, 0), Rbf(tb + 5, 1)], [RKh(tb + 5, 0), RKh(tb + 5, 1)]
                o.Sb = [Rbf(tb + 6, 0), Rbf(tb + 6, 1), Rbf(tb + 7, 0)]
                o.Sbk = [RKh(tb + 6, 0), RKh(tb + 6, 1), RKh(tb + 7, 0)]
                o.a, o.ak = [Rbf(tb + 8, 0), Rbf(tb + 8, 1)], [RKh(tb + 8, 0), RKh(tb + 8, 1)]
                o.zb = [2 * hh, 2 * hh + 1]
                o.tbk = 4 + hh
                o.ob = 6 + hh
                sts.append(o)
            for o in sts:
                mset("pool", o.S, 0.0, [], o.Sk)
                for t in range(3):
                    mset("pool", o.Sb[t], 0.0, [], o.Sbk[t])
                mm(bank(o.ob), negones, o.Sb[0], True, False, o.Sbk[0] + ["cm"], [PK(o.ob)])

            def geom(n):
                g = glist[n]
                if g < 0:
                    return 16, 0, 0, -1, 0
                r = g - 8 * s
                return 128, NMETA + 128 * g, 1 + g, r, (128 * (r // 2) if r >= 0 else 0)

            def stageA(n):
                kr, kc0, blk, r, c0 = geom(n)
                for o in sts:
                    zb = o.zb[n % 2]
                    mm(bank(zb)[0:kr, c0:512], KTs[o.p0:o.p1, kc0:kc0 + kr], QTs[o.p0:o.p1, q0 + c0:q0 + 512],
                       True, r < 0, KTkeys[0] + [("QT", 0, sl), ("KTone", 0)], [PK(zb)])
                    if r >= 0:
                        mm(bank(zb)[:, c0:c0 + 128], ident, masks[:, 1024 + r * 128:1024 + (r + 1) * 128],
                           False, True, ["cm", "masks"], [PK(zb)])
                for o in sts:
                    zb = o.zb[n % 2]
                    act(o.e[n % 2][0:kr, c0:512], bank(zb)[0:kr, c0:512], AF.Exp, [PK(zb)], o.ek[n % 2], scale=0.125)
                for o in sts:
                    act(o.sp[n % 2][0:kr, c0:512], o.e[n % 2][0:kr, c0:512], AF.Ln, o.ek[n % 2], o.spk[n % 2],
                        bias=1.0)
                if n + 1 < nst:
                    for o in sts:
                        tt("pool", o.Sb[(n + 1) % 3][:, c0:512], o.S[:, c0:512], o.sp[n % 2][:, c0:512],
                           ALU.add, o.Sk + o.spk[n % 2], o.Sbk[(n + 1) % 3])
                        tt("dve", o.S[:, c0:512], o.S[:, c0:512], o.sp[n % 2][:, c0:512],
                           ALU.add, o.Sk + o.spk[n % 2], o.Sk)

            def stageB(n):
                kr, kc0, blk, r, c0 = geom(n)
                last = n == nst - 1
                for o in sts:
                    mm(bank(o.tbk)[0:kr, c0:512], negtri[0:kr, 0:kr], o.sp[n % 2][0:kr, c0:512], True, False,
                       o.spk[n % 2] + ["cm"], [PK(o.tbk)])
                    mm(bank(o.tbk)[0:kr, c0:512], negones[:, 0:kr], o.Sb[n % 3][:, c0:512], False, True,
                       o.Sbk[n % 3] + ["cm"], [PK(o.tbk)])
                for o in sts:
                    act(o.x[n % 2][0:kr, c0:512], bank(o.tbk)[0:kr, c0:512], AF.Exp, [PK(o.tbk)], o.xk[n % 2])
                for o in sts:
                    tt("dve", o.a[n % 2][0:kr, c0:512], o.e[n % 2][0:kr, c0:512], o.x[n % 2][0:kr, c0:512], ALU.mult,
                       o.ek[n % 2] + o.xk[n % 2], o.ak[n % 2])
                for o in sts:
                    mm(bank(o.ob)[:, c0:512], V3[0:kr, blk, :], o.a[n % 2][0:kr, c0:512], False, last,
                       o.ak[n % 2] + [Vkeys[blk]], [PK(o.ob)])

            stageA(0)
            for n in range(nst):
                if n + 1 < nst:
                    stageA(n + 1)
                stageB(n)
            for o in sts:
                tt("dve", B["YT"][u][o.p0:o.p1, q0:q0 + 512], bank(o.ob)[o.p0:o.p1, :], GT[o.p0:o.p1, q0:q0 + 512],
                   ALU.mult, [PK(o.ob), ("GT", sl)], [("YT", u, sl, o.hh)])

        def tail(P, B):
            KTkeys, Vkeys = B["KTkeys"], B["Vkeys"]
            A.release(mark_unit)
            wtb = A.alloc(4096, BF16)
            wm3 = wtb[:, 0:2048].rearrange("p (c n) -> p c n", c=8)
            wd3 = wtb[:, 2048:3072].rearrange("p (u n) -> p u n", u=8)
            ws3 = wtb[:, 3072:4096].rearrange("p (u n) -> p u n", u=8)
            mT = A.alloc(8 * 1024, BF16).rearrange("p (c t) -> p c t", c=8)
            wob = A.alloc(8 * 1024, BF16)
            wo3 = wob.rearrange("p (c n) -> p c n", c=8)
            ytall = [("YT", u, sl, hh) for u in range(16) for sl in range(2) for hh in range(2)]
            free_keys = ytall + ["wb", ("KTone", 0), ("KTone", 1)] + KTkeys[0] + KTkeys[1] + Vkeys + \
                [("QT", c, sl) for c in range(2) for sl in range(2)] + [("GT", 0), ("GT", 1)]
            mset("dve", dummy[:, 4:5], 0.0, free_keys, ["YTall", "tailfree"] + free_keys)
            mset("pool", dummy[:, 5:6], 0.0, ["tailfree"], ["tailfree2"])
            dma("pool", "dw2", wob, wo, ["tailfree2"], ["wob"])
            load_gb()
            for cc in range(8):
                dma("pool", "dw1", wtb, wt[cc], ["tailfree2"], ["wtb"])
                for sl in range(2):
                    q0 = sl * 512
                    bd, bs, bmd, bms = 0, 1, 2, 3
                    for uu in range(8):
                        mm(bank(bd), wd3[:, uu, :], YT[uu][:, q0:q0 + 512], uu == 0, uu == 7, ["wtb", "YTall"], [PK(bd)])
                    for uu in range(8):
                        mm(bank(bs), ws3[:, uu, :], YT[8 + uu][:, q0:q0 + 512], uu == 0, uu == 7,
                           ["wtb", "YTall"], [PK(bs)])
                    for ch in range(8):
                        mm(bank(bmd), wm3[:, ch, 0:128], hqT[:, ch, q0:q0 + 512], ch == 0, ch == 7,
                           ["wtb", "hqTall"], [PK(bmd)])
                    for ch in range(8):
                        mm(bank(bms), wm3[:, ch, 128:256], hqT[:, ch, q0:q0 + 512], ch == 0, ch == 7,
                           ["wtb", "hqTall"], [PK(bms)])
                    sd, ss = Rf[6], Rf[7]
                    for (bm, sgt, rk, br) in ((bmd, sd, 6, 0), (bms, ss, 7, 1)):
                        act(sgt, bank(bm), AF.Exp, [PK(bm), "bg"], RK(rk),
                            bias=bg[:, br * 8 + cc:br * 8 + cc + 1], scale=-1.0)
                        tsadd("dve", sgt, sgt, 1.0, RK(rk), RK(rk))
                        recip(sgt, sgt, RK(rk), RK(rk))
                    tt("dve", sd, bank(bd), sd, ALU.mult, [PK(bd)] + RK(6), RK(6))
                    tt("dve", ss, bank(bs), ss, ALU.mult, [PK(bs)] + RK(7), RK(7))
                    tt("pool", mT[:, cc, q0:q0 + 512], sd, ss, ALU.add, RK(6) + RK(7), [("mT", cc, sl)])
            mkeys = [("mT", cc, sl) for cc in range(8) for sl in range(2)]
            for jb in range(8):
                i = jb % 2
                r0 = (8 * P + jb) * 128
                xa, xb2 = xbuf[i]
                load_x(i, xq[r0:r0 + 128, :], 128)
                ln_core(i, 128, 0, (xa, xb2), (RK(2 * i), RK(2 * i + 1)))
                for hf in range(2):
                    b = 4 + 2 * i + hf
                    for cc in range(8):
                        mm(bank(b), mT[:, cc, jb * 128:(jb + 1) * 128], wo3[:, cc, hf * 512:(hf + 1) * 512],
                           cc == 0, cc == 7, mkeys + ["wob"], [PK(b)])
                    xs = (xa, xb2)[hf]
                    stt("dve", xs, xs, DN_ALPHA, bank(b), ALU.mult, ALU.add, RK(2 * i + hf) + [PK(b)], RK(2 * i + hf))
                ln_core(i, 128, 2, (xa, xb2), (RK(2 * i), RK(2 * i + 1)))
                dma("sp", "do%d" % i, out[r0:r0 + 128, 0:512], xa, RK(2 * i), [])
                dma("sp", "do%d" % i, out[r0:r0 + 128, 512:1024], xb2, RK(2 * i + 1), [])
            tk = mkeys + ["wob", "wtb", "YTall", "hqTall"]
            mset("dve", dummy[:, 6:7], 0.0, tk, ["tailend"] + tk)

        for P in passes:
            nreal = 16 * (P + 1)
            ncols = NMETA + 128 * nreal
            A.release(mark_unit)
            B = {"nreal": nreal, "ncols": ncols, "YT": YT}
            B["wb"] = A.alloc(8 * 512, BF16)
            B["wb3"] = B["wb"].rearrange("p (c n) -> p c n", c=8)
            B["KTa"] = A.alloc(L, BF16)
            B["KTb"] = A.alloc(L, BF16)
            B["V3"] = A.alloc(33 * 128, BF16).rearrange("p (b n) -> p b n", b=33)
            B["QTa"] = A.alloc(1024, BF16)
            B["QTb"] = A.alloc(1024, BF16)
            B["GT"] = A.alloc(1024, F32)
            B["ktiles"] = [(a, min(512, ncols - a)) for a in range(0, ncols, 512)]
            B["KTkeys"] = [[("KT", c, a) for (a, n) in B["ktiles"]] for c in range(2)]
            B["Vkeys"] = [("V", b) for b in range(1 + nreal)]
            if P != passes[0]:
                allk = ["wb", ("KTone", 0), ("KTone", 1)] + B["KTkeys"][0] + B["KTkeys"][1] + B["Vkeys"] + \
                    [("QT", c, sl) for c in range(2) for sl in range(2)] + [("GT", 0), ("GT", 1)]
                allk += [("YT", u_, sl_, hh_) for u_ in range(16) for sl_ in range(2) for hh_ in range(2)]
                allk += [("hqT", k_) for k_ in range(8)]
                mset("dve", dummy[:, 7:8], 0.0, ["tailend"], allk)

            for jb in range(8):
                r0 = (8 * P + jb) * 128
                ln_block_T(xq[r0:r0 + 128, :], 128, hqT, jb * 128, [("hqT", jb), "hqTall"])
            join([("hqT", k) for k in range(8)], "hqTall", 2)

            for u in units:
                unit_proj(P, u, B)
                for sl in range(2):
                    if u < 8:
                        diff_slot(P, u, sl, B)
                    else:
                        sb_slot(P, u, sl, B)

            if stage == "U":
                if P == passes[0]:
                    for k, u in enumerate(units[:2]):
                        for hf in range(2):
                            tmp = Rf[2 * k + hf]
                            vcopy("dve", tmp, YT[u][:, hf * 512:(hf + 1) * 512],
                                  [("YT", u, hf, 0), ("YT", u, hf, 1)], RK(2 * k + hf))
                            dma("sp", "do0", dbg[:, k * 1024 + hf * 512:k * 1024 + hf * 512 + 512], tmp,
                                RK(2 * k + hf), [])
                continue
            tail(P, B)

        T.emit(nc, sems, block)
    return nc


def _bf16(a):
    return np.asarray(a, dtype=np.float32).astype(ml_dtypes.bfloat16)


def _const_tables(p):
    k = np.arange(128)[:, None]
    i = np.arange(128)[None, :]
    diagD = np.where(k > i, NEG, 0.0).astype(np.float32)
    diagS = np.where(k >= i, NEG, 0.0).astype(np.float32)
    full = np.full((128, 128), NEG, np.float32)
    zero = np.zeros((128, 128), np.float32)
    m = np.zeros((128, 2, 8, 128), np.float32)
    for r in range(8):
        for bi, dg in enumerate((diagD, diagS)):
            if p == 0:
                m[:, bi, r, :] = dg if r % 2 == 0 else full
            else:
                m[:, bi, r, :] = zero if r % 2 == 0 else dg
    cmask = _bf16(m.reshape(128, 2048))
    bias = np.zeros((128, 8, 36), np.float32)
    kk = np.arange(128, dtype=np.float64)
    for h in range(8):
        for idx in range(32):
            delta = idx - 24
            bias[:, h, idx] = SLOPES[h] * (128.0 * (delta - p) + kk)
        for s in range(4):
            bias[:, h, 32 + s] = SLOPES[h] * (kk - 16.0 - 128.0 * (8 * s + p))
    cbias = bias.reshape(128, 8 * 36)
    col = np.arange(512)
    pos = 256.0 * (col // 128) + (col % 128)
    caug = _bf16(np.stack([-8.0 * SLOPES[h] * pos for h in range(8)]))
    j = np.arange(128)[:, None]
    kc = np.arange(128)[None, :]
    ident = (j == kc).astype(np.float32)
    negtri = -(j >= kc).astype(np.float32)
    negones = -np.ones((128, 128), np.float32)
    ones = np.ones((128, 128), np.float32)
    cmat = _bf16(np.concatenate([ident, negtri, negones, ones], axis=1))
    return cmask, cbias, caug, cmat


def _pc(w):
    n = w.shape[1]
    return np.ascontiguousarray(w.reshape(8, 128, n).transpose(1, 0, 2).reshape(128, 8 * n))


def make_in_maps(inputs):
    x = np.asarray(inputs["x"], np.float32)
    w_in = np.asarray(inputs["w_in"], np.float32)[0]
    secs = np.split(w_in, 10, axis=1)
    units = []
    for u in range(16):
        base = 0 if u < 8 else 4
        j = u % 8
        cols = [secs[base + t][:, j * 128:(j + 1) * 128] for t in range(4)]
        units.append(_pc(np.concatenate(cols, axis=1)))
    wu = np.stack(units)
    wbd = np.asarray(inputs["w_br_diff"], np.float32)[0]
    wbs = np.asarray(inputs["w_br_sb"], np.float32)[0]
    wt = np.zeros((8, 128, 4096), np.float32)
    for cc in range(8):
        cs = slice(cc * 128, (cc + 1) * 128)
        mcols = np.concatenate([secs[8][:, cs], secs[9][:, cs]], axis=1)
        wt[cc, :, 0:2048] = _pc(mcols)
        wt[cc, :, 2048:3072] = _pc(wbd[:, cs])
        wt[cc, :, 3072:4096] = _pc(wbs[:, cs])
    wo = _pc(np.asarray(inputs["w_out"], np.float32)[0])
    lnv = np.stack([np.asarray(inputs["emb_ln_g"], np.float32), np.asarray(inputs["emb_ln_b"], np.float32),
                    np.asarray(inputs["ln_g"], np.float32)[0], np.asarray(inputs["ln_b"], np.float32)[0]])
    bgate = np.ascontiguousarray(
        np.asarray(inputs["b_gate"], np.float32)[0].reshape(2, 8, 128).transpose(2, 0, 1).reshape(128, 16))
    lamv = np.asarray(inputs["diff_lambda"], np.float32)[0].reshape(1, 256)
    subg = np.asarray(inputs["diff_subln_g"], np.float32)[0].reshape(128, 1)
    meta = np.asarray(inputs["meta_tokens"], np.float32)
    maps = []
    for c in range(NCORES):
        b, p = c // 2, c % 2
        cmask, cbias, caug, cmat = _const_tables(p)
        xq = np.ascontiguousarray(x[b].reshape(16, 2, 128, D)[:, p].reshape(SEQ // 2, D))
        maps.append({"xf": np.ascontiguousarray(x[b]), "xq": xq, "meta": meta, "lnv": lnv, "wu": wu,
                     "wt": wt, "wo": wo, "bgate": bgate, "lamv": lamv, "subg": subg,
                     "cmask": cmask, "cbias": cbias, "caug": caug, "cmat": cmat})
    return maps


def kernel(**inputs):
    nc = build_program("full")
    maps = make_in_maps(inputs)
    res = run_bass_kernel_spmd(nc, maps, core_ids=list(range(NCORES)))
    outp = np.zeros((4, SEQ, D), np.float32)
    for c in range(NCORES):
        b, p = c // 2, c % 2
        outp[b].reshape(16, 2, 128, D)[:, p] = np.asarray(res.results[c]["out"]).reshape(16, 128, D)
    return outp
```

```python
import numpy as np
import ml_dtypes
import concourse.bass as bass
import concourse.mybir as mybir
from concourse.bass_utils import run_bass_kernel_spmd

F32 = mybir.dt.float32
BF16 = mybir.dt.bfloat16
AF = mybir.ActivationFunctionType
ALU = mybir.AluOpType

D = 1024
SEQ = 4096
NMETA = 16
L = SEQ + NMETA
NCORES = 8
NEG = -30000.0
LN_EPS = 1e-5
RMS_EPS = 1e-5
DN_ALPHA = 2.0 ** 0.25
LAM_INIT = 0.8 - 0.6 * 1.0
SLOPES = [2.0 ** (-(h + 1)) for h in range(8)]


class _Op:
    __slots__ = ("fn", "eng", "lane", "lidx", "deps", "dma", "signal")


class Tracker:
    COMPUTE = ("pe", "act", "dve", "pool")

    def __init__(self):
        self.streams = {e: [] for e in ("pe", "act", "dve", "pool", "sp")}
        self.lane_ops = {}
        self.last_w = {}
        self.readers = {}

    def _add(self, eng, lane, fn, reads, writes, dma):
        op = _Op()
        op.fn, op.eng, op.lane, op.dma, op.signal = fn, eng, lane, dma, dma
        lst = self.lane_ops.setdefault(lane, [])
        lst.append(op)
        op.lidx = len(lst)
        deps = {}

        def add(d, raw=False):
            if d is None:
                return
            if d.lane == lane and not dma and lane == "pe":
                return
            cur = deps.get(d.lane)
            if cur is None or cur.lidx < d.lidx:
                deps[d.lane] = d

        for k in reads:
            add(self.last_w.get(k), True)
        for k in writes:
            add(self.last_w.get(k))
            for r in self.readers.get(k, {}).values():
                add(r)
        if dma and len(lst) > 1:
            add(lst[-2])
        op.deps = list(deps.values())
        for k in reads:
            self.readers.setdefault(k, {})[lane] = op
        for k in writes:
            self.last_w[k] = op
            self.readers[k] = {}
        self.streams[eng].append(op)
        return op

    def op(self, eng, fn, reads=(), writes=()):
        return self._add(eng, eng, fn, reads, writes, False)

    def dma(self, eng, lane, fn, reads=(), writes=()):
        return self._add(eng, lane, fn, reads, writes, True)

    def emit(self, nc, sems, block):
        for e, ops in self.streams.items():
            for op in ops:
                for d in op.deps:
                    d.signal = True
        val = {}
        for lane, ops in self.lane_ops.items():
            c = 0
            for op in ops:
                if op.signal:
                    c += 16 if op.dma else 1
                val[(lane, op.lidx)] = c
        final = {lane: val[(lane, len(ops))] for lane, ops in self.lane_ops.items()
                 if ops and ops[-1].dma}
        streams = self.streams

        def run(engname, eng):
            waited = {}
            for op in streams[engname]:
                for d in op.deps:
                    v = val[(d.lane, d.lidx)]
                    if waited.get(d.lane, 0) < v:
                        eng.wait_ge(sems[d.lane], v)
                        waited[d.lane] = v
                ins = op.fn(eng)
                if op.signal:
                    ins.then_inc(sems[op.lane], 16 if op.dma else 1)
            if engname == "sp":
                for lane, v in final.items():
                    eng.wait_ge(sems[lane], v)

        @block.tensor
        def _(e):
            run("pe", e)

        @block.scalar
        def _(e):
            run("act", e)

        @block.vector
        def _(e):
            run("dve", e)

        @block.gpsimd
        def _(e):
            run("pool", e)

        @block.sync
        def _(e):
            run("sp", e)


class Arena:
    def __init__(self, base_ap, nwords):
        self.base = base_ap
        self.n = nwords
        self.top = 0

    def mark(self):
        return self.top

    def release(self, m):
        self.top = m

    def alloc(self, nelem, dtype=F32, parts=128):
        if dtype == F32:
            w = nelem
        else:
            w = (nelem + 1) // 2
        w = (w + 7) // 8 * 8
        a = self.top
        self.top += w
        assert self.top <= self.n, f"arena overflow {self.top} > {self.n}"
        ap = self.base[0:parts, a:a + w]
        if dtype != F32:
            ap = ap.bitcast(dtype)
        return ap[:, 0:nelem]


def build_program(stage="full", units=None, passes=(0, 1)):
    nc = bass.Bass("TRN2", target_bir_lowering=False)
    T = Tracker()
    if units is None:
        units = list(range(16))
    AX = mybir.AxisListType.X

    def din(name, shape, dt=F32):
        return nc.dram_tensor(name, list(shape), dt, kind="ExternalInput").ap()

    xf = din("xf", [SEQ, D])
    xq = din("xq", [SEQ // 2, D])
    meta = din("meta", [NMETA, D])
    lnv = din("lnv", [4, D])
    wu = din("wu", [16, 128, 8 * 512])
    wt = din("wt", [8, 128, 4096])
    wo = din("wo", [128, 8 * 1024])
    bgate = din("bgate", [128, 16])
    lamv = din("lamv", [1, 256])
    subg = din("subg", [128, 1])
    cmask = din("cmask", [128, 2 * 8 * 128], BF16)
    cbias = din("cbias", [128, 8 * 36])
    caug = din("caug", [8, 512], BF16)
    cmat = din("cmat", [128, 4 * 128], BF16)
    out = nc.dram_tensor("out", [SEQ // 2, D], F32, kind="ExternalOutput").ap()
    dbg = None
    if stage != "full":
        dbg = nc.dram_tensor("dbg", [128, 8192], F32, kind="ExternalOutput").ap()

    NW = 53000
    lanes = ["pe", "act", "dve", "pool", "dx0", "dx1", "dw0", "dw1", "dw2", "dc", "do0", "do1", "dq"]
    import contextlib
    with contextlib.ExitStack() as es:
        arena_t = es.enter_context(nc.sbuf_tensor("arena", [128, NW], F32))
        ps_t = es.enter_context(nc.psum_tensor("ps", [128, 8 * 512], F32))
        sems = {ln: es.enter_context(nc.semaphore("s_" + ln)) for ln in lanes}
        block = es.enter_context(nc.Block())
        A = Arena(arena_t[:], NW)
        PS = ps_t[:]

        def bank(b):
            return PS[:, b * 512:(b + 1) * 512]

        def PK(b):
            return ("ps", b)

        def mm(out_, lhsT, rhs, start, stop, reads, writes):
            T.op("pe", lambda e: e.matmul(out=out_, lhsT=lhsT, rhs=rhs, start=start, stop=stop),
                 reads=reads, writes=writes)

        def tr(out_, in_, idn, reads, writes):
            T.op("pe", lambda e: e.transpose(out=out_, in_=in_, identity=idn), reads=reads, writes=writes)

        def act(out_, in_, func, reads, writes, bias=None, scale=None):
            kw = {}
            if bias is not None:
                kw["bias"] = bias
            if scale is not None:
                kw["scale"] = scale
            T.op("act", lambda e: e.activation(out=out_, in_=in_, func=func, **kw), reads=reads, writes=writes)

        def acopy(out_, in_, reads, writes):
            T.op("act", lambda e: e.copy(out=out_, in_=in_), reads=reads, writes=writes)

        def vcopy(eng, out_, in_, reads, writes):
            T.op(eng, lambda e: e.tensor_copy(out=out_, in_=in_), reads=reads, writes=writes)

        def tt(eng, out_, in0, in1, op, reads, writes):
            T.op(eng, lambda e: e.tensor_tensor(out=out_, in0=in0, in1=in1, op=op), reads=reads, writes=writes)

        def ts(eng, out_, in0, s1, s2, op0, op1, reads, writes):
            T.op(eng, lambda e: e.tensor_scalar(out=out_, in0=in0, scalar1=s1, scalar2=s2, op0=op0, op1=op1),
                 reads=reads, writes=writes)

        def tsadd(eng, out_, in0, s1, reads, writes):
            T.op(eng, lambda e: e.tensor_scalar_add(out=out_, in0=in0, scalar1=s1), reads=reads, writes=writes)

        def tsmul(eng, out_, in0, s1, reads, writes):
            T.op(eng, lambda e: e.tensor_scalar_mul(out=out_, in0=in0, scalar1=s1), reads=reads, writes=writes)

        def stt(eng, out_, in0, scalar, in1, op0, op1, reads, writes):
            T.op(eng, lambda e: e.scalar_tensor_tensor(out=out_, in0=in0, scalar=scalar, in1=in1, op0=op0, op1=op1),
                 reads=reads, writes=writes)

        def recip(out_, in_, reads, writes):
            T.op("dve", lambda e: e.reciprocal(out=out_, in_=in_), reads=reads, writes=writes)

        def mset(eng, ap, val, reads, writes):
            T.op(eng, lambda e: e.memset(ap, val), reads=reads, writes=writes)

        def dma(eng, lane, out_, in_, reads, writes):
            if eng == "pool":
                T.dma(eng, lane, lambda e: e.dma_start(out=out_, in_=in_, max_dma_last_dim=4096),
                      reads=reads, writes=writes)
            else:
                T.dma(eng, lane, lambda e: e.dma_start(out=out_, in_=in_), reads=reads, writes=writes)

        pbctr = [0]

        def nextbank():
            pbctr[0] = (pbctr[0] + 1) % 8
            return pbctr[0]

        evctr = [0]

        def evac(out_ap, in_ap, reads, writes):
            evctr[0] += 1
            if evctr[0] % 2:
                acopy(out_ap, in_ap, reads, writes)
            else:
                vcopy("dve", out_ap, in_ap, reads, writes)

        cm = A.alloc(512, BF16)
        ident, negtri, negones, ones = (cm[:, i * 128:(i + 1) * 128] for i in range(4))
        masks = A.alloc(2048, BF16)
        biasT = A.alloc(8 * 36, F32)
        gb = A.alloc(4 * D, F32)
        bg = A.alloc(16, F32)
        sg = A.alloc(8, F32)
        onesf = A.alloc(128, F32)
        lamt = A.alloc(640, F32)
        neglam = A.alloc(8, F32)
        dummy = A.alloc(8, F32)
        dma("sp", "dc", cm, cmat, [], ["cm"])
        dma("sp", "dc", masks, cmask, [], ["masks"])
        dma("sp", "dc", biasT, cbias, [], ["biasT"])
        dma("sp", "dc", bg, bgate, [], ["bg"])
        dma("sp", "dc", sg[:, 0:1], subg, [], ["sg"])
        l0 = lamt[0:1, :]
        dma("sp", "dc", l0[:, 0:256], lamv, [], ["lamt"])
        mset("pool", onesf, 1.0 / 128.0, [], ["onesf"])
        tsmul("dve", sg[:, 1:2], sg[:, 0:1], 1.0 - LAM_INIT, ["sg"], ["sgs"])
        tsmul("dve", bg, bg, -1.0, ["bg"], ["bg"])
        tt("dve", l0[:, 256:320], l0[:, 0:64], l0[:, 64:128], ALU.mult, ["lamt"], ["lam1"])
        tt("dve", l0[:, 320:384], l0[:, 128:192], l0[:, 192:256], ALU.mult, ["lamt"], ["lam2"])
        T.op("dve", lambda e: e.reduce_sum(out=l0[:, 384:385], in_=l0[:, 256:320], axis=AX),
             reads=["lam1"], writes=["lam3a"])
        T.op("dve", lambda e: e.reduce_sum(out=l0[:, 385:386], in_=l0[:, 320:384], axis=AX),
             reads=["lam2"], writes=["lam3b"])
        act(l0[:, 386:388], l0[:, 384:386], AF.Exp, ["lam3a", "lam3b"], ["lam4"])
        stt("dve", l0[:, 388:389], l0[:, 387:388], -LAM_INIT, l0[:, 386:387], ALU.add, ALU.subtract,
            ["lam4"], ["lam5"])
        mset("pool", l0[:, 400:528], 1.0, [], ["one1"])
        mm(bank(0)[:, 0:1], l0[:, 400:528], l0[:, 388:389], True, True, ["one1", "lam5"], [PK(0)])
        vcopy("dve", neglam[:, 0:1], bank(0)[:, 0:1], [PK(0)], ["neglam"])

        hT = A.alloc(8 * L, BF16).rearrange("p (c t) -> p c t", c=8)
        NR = 10
        Rf = [A.alloc(512, F32) for _ in range(NR)]

        Rf = Rf + [gb[:, j * 512:(j + 1) * 512] for j in range(8)]

        def RK(i):
            if i >= 10:
                return [("gbR", i - 10, 0), ("gbR", i - 10, 1)]
            return [("R", i, 0), ("R", i, 1)]

        def RKh(i, half):
            return [RK(i)[half]]

        def gbk(i):
            return [("gb", i)] + RK(10 + 2 * i) + RK(11 + 2 * i)

        def load_gb():
            for i in range(4):
                dma("sp", "dc", gb[:, i * D:(i + 1) * D], lnv[i:i + 1, :].partition_broadcast(128), [], gbk(i))

        def Rbf(i, half):
            return Rf[i].bitcast(BF16)[:, half * 512:(half + 1) * 512]

        xbuf = [(Rf[0], Rf[1]), (Rf[2], Rf[3])]
        load_gb()
        stat = [A.alloc(16, F32), A.alloc(16, F32)]
        lncount = [0]

        def ln_core(i, rows, gi, out_tiles, out_keys):
            xs = (xbuf[i][0][0:rows], xbuf[i][1][0:rows])
            xk = RK(2 * i) + RK(2 * i + 1)
            st = stat[i][0:rows]
            sk = ("st", i)
            for hf in range(2):
                xh = xs[hf]
                so = st[:, 6 * hf:6 * hf + 6]
                T.op("dve", (lambda xh, so: (lambda e: e.bn_stats(out=so, in_=xh)))(xh, so), reads=xk, writes=[sk])
            T.op("dve", lambda e: e.bn_aggr(out=st[:, 12:14], in_=st[:, 0:12]), reads=[sk], writes=[sk])
            tsadd("dve", st[:, 15:16], st[:, 13:14], LN_EPS, [sk], [sk])
            act(st[:, 15:16], st[:, 15:16], AF.Ln, [sk], [sk])
            act(st[:, 14:15], st[:, 15:16], AF.Exp, [sk], [sk], scale=-0.5)
            for hf in range(2):
                hk = RK(2 * i + hf)
                ts("dve", xs[hf], xs[hf], st[:, 12:13], st[:, 14:15], ALU.subtract, ALU.mult, xk + [sk], hk)
                tt("pool", xs[hf], xs[hf], gb[0:rows, gi * D + hf * 512: gi * D + hf * 512 + 512], ALU.mult,
                   hk + gbk(gi), hk)
                tt("dve", out_tiles[hf], xs[hf],
                   gb[0:rows, (gi + 1) * D + hf * 512:(gi + 1) * D + hf * 512 + 512], ALU.add,
                   hk + gbk(gi + 1), out_keys[hf])

        def load_x(i, src_ap, rows):
            dma("sp", "dx%d" % i, xbuf[i][0][0:rows], src_ap[:, 0:512], [], RK(2 * i))
            dma("sp", "dx%d" % i, xbuf[i][1][0:rows], src_ap[:, 512:1024], [], RK(2 * i + 1))

        def ln_block_T(src_ap, rows, dstT, col0, key):
            i = lncount[0] % 2
            lncount[0] += 1
            load_x(i, src_ap, rows)
            hb = Rf[4 + i].bitcast(BF16)[0:rows]
            ln_core(i, rows, 0, (hb[:, 0:512], hb[:, 512:1024]), ([("R", 4 + i, 0)], [("R", 4 + i, 1)]))
            pb = nextbank()
            pst = bank(pb).bitcast(BF16).rearrange("p (c t) -> p c t", c=8)
            for c in range(8):
                tr(pst[:, c, 0:rows], hb[:, c * 128:(c + 1) * 128], ident[0:rows, 0:rows],
                   RK(4 + i) + ["cm"], [PK(pb)])
            evac(dstT[:, :, col0:col0 + rows], pst[:, :, 0:rows], [PK(pb)], key if isinstance(key, list) else [key])

        def join(keys, name, slot):
            mset("dve", dummy[:, slot:slot + 1], 0.0, keys, [name + "_d"])
            acopy(dummy[:, slot + 1:slot + 2], dummy[:, slot:slot + 1], keys + [name + "_d"], [name])

        ln_block_T(meta, NMETA, hT, 0, ("hT", 0))
        for g in range(32):
            ln_block_T(xf[g * 128:(g + 1) * 128, :], 128, hT, NMETA + g * 128, ("hT", 1 + g))
        join([("hT", k) for k in range(33)], "hTall", 0)

        hqT = A.alloc(8 * 1024, BF16).rearrange("p (c t) -> p c t", c=8)
        YT = [A.alloc(1024, BF16) for _ in range(16)]
        mark_unit = A.mark()

        def unit_proj(P, u, B):
            is_diff = u < 8
            h = u % 8
            ui = B["order"].index(u)
            wb, wb3, wkey = B["wbs"][ui % 2]
            KTa, KTb, V3, QTa, QTb, GT = B["KTa"], B["KTb"], B["V3"], B["QTa"], B["QTb"], B["GT"]
            nreal, ncols, ktiles, KTkeys, Vkeys = B["nreal"], B["ncols"], B["ktiles"], B["KTkeys"], B["Vkeys"]
            if ui == 0:
                dma("pool", "dw0", wb, wu[u], [], [wkey])
            if is_diff:
                for c, KTc in enumerate((KTa, KTb)):
                    mset("pool", KTc[64:65, 0:ncols], 1.0, [], [("KTone", c)] + KTkeys[c])
                    for ti, (a, n) in enumerate(ktiles):
                        b = nextbank()
                        for ch in range(8):
                            mm(bank(b)[0:64, 0:n], wb3[:, ch, 128 + 64 * c:192 + 64 * c], hT[:, ch, a:a + n],
                               ch == 0, ch == 7, [wkey, "hTall"], [PK(b)])
                        evac(KTc[0:64, a:a + n], bank(b)[0:64, 0:n], [PK(b)], [KTkeys[c][ti]])
            else:
                for ti, (a, n) in enumerate(ktiles):
                    b = nextbank()
                    for ch in range(8):
                        mm(bank(b)[:, 0:n], wb3[:, ch, 128:256], hT[:, ch, a:a + n], ch == 0, ch == 7,
                           [wkey, "hTall"], [PK(b)])
                    evac(KTa[:, a:a + n], bank(b)[:, 0:n], [PK(b)], [KTkeys[0][ti], ("KTone", 0)])
            b = nextbank()
            for ch in range(8):
                mm(bank(b)[0:16, 0:128], hT[:, ch, 0:16], wb3[:, ch, 256:384], ch == 0, ch == 7,
                   [wkey, "hTall"], [PK(b)])
            evac(V3[0:16, 0, :], bank(b)[0:16, 0:128], [PK(b)], [Vkeys[0]])
            for a4 in range(nreal // 4):
                b = nextbank()
                for j in range(4):
                    c0 = NMETA + 128 * (4 * a4 + j)
                    for ch in range(8):
                        mm(bank(b)[:, j * 128:(j + 1) * 128], hT[:, ch, c0:c0 + 128], wb3[:, ch, 256:384],
                           ch == 0, ch == 7, [wkey, "hTall"], [PK(b)])
                evac(V3[:, 1 + 4 * a4:5 + 4 * a4, :], bank(b).rearrange("p (j n) -> p j n", j=4),
                     [PK(b)], [Vkeys[1 + 4 * a4 + j] for j in range(4)])
            if is_diff:
                for c, QTc in enumerate((QTa, QTb)):
                    for sl in range(2):
                        dma("sp", "dq", QTc[64:65, sl * 512:(sl + 1) * 512], caug[h:h + 1, :], [], [("QT", c, sl)])
                    for sl in range(2):
                        b = nextbank()
                        for ch in range(8):
                            mm(bank(b)[0:64, :], wb3[:, ch, 64 * c:64 * c + 64], hqT[:, ch, sl * 512:(sl + 1) * 512],
                               ch == 0, ch == 7, [wkey, "hqTall"], [PK(b)])
                        evac(QTc[0:64, sl * 512:(sl + 1) * 512], bank(b)[0:64, :], [PK(b)], [("QT", c, sl)])
            else:
                for sl in range(2):
                    b = nextbank()
                    for ch in range(8):
                        mm(bank(b), wb3[:, ch, 0:128], hqT[:, ch, sl * 512:(sl + 1) * 512], ch == 0, ch == 7,
                           [wkey, "hqTall"], [PK(b)])
                    evac(QTa[:, sl * 512:(sl + 1) * 512], bank(b), [PK(b)], [("QT", 0, sl)])
            for sl in range(2):
                b = nextbank()
                for ch in range(8):
                    mm(bank(b), wb3[:, ch, 384:512], hqT[:, ch, sl * 512:(sl + 1) * 512], ch == 0, ch == 7,
                       [wkey, "hqTall"], [PK(b)])
                tg = Rf[8 + sl]
                act(tg, bank(b), AF.Exp, [PK(b)], RK(8 + sl), scale=-1.0)
                tsadd("dve", tg, tg, 1.0, RK(8 + sl), RK(8 + sl))
                recip(tg, tg, RK(8 + sl), RK(8 + sl))
                tt("dve", GT[:, sl * 512:(sl + 1) * 512], bank(b), tg, ALU.mult, RK(8 + sl) + [PK(b)], [("GT", sl)])
            if ui + 1 < len(B["order"]):
                nwb, _, nkey = B["wbs"][(ui + 1) % 2]
                dma("pool", "dw0", nwb, wu[B["order"][ui + 1]], [], [nkey])

        def diff_slot(P, u, sl, B):
            h = u % 8
            KT = (B["KTa"], B["KTb"])
            QT = (B["QTa"], B["QTb"])
            V3, GT, KTkeys, Vkeys = B["V3"], B["GT"], B["KTkeys"], B["Vkeys"]
            s = 2 * P + sl
            q0 = sl * 512
            gmax = 8 * s + 7
            steps = [0, -1] + list(range(1, gmax + 1))
            nst = len(steps)
            Ob = (4, 5)
            Rb = (6, 7)
            items = []
            for ti, g in enumerate(steps):
                first, last = ti == 0, ti == nst - 1
                if g < 0:
                    kr, kc0, blk, r, bidx = 16, 0, 0, -1, 32 + s
                else:
                    kr, kc0, blk, r, bidx = 128, NMETA + 128 * g, 1 + g, g - 8 * s, (g - 8 * s) + 24
                c0 = 128 * (r // 2) if r >= 0 else 0
                for c in range(2):
                    items.append((c, first, last, kr, kc0, blk, r, bidx, c0))

            def qk(n):
                c, first, last, kr, kc0, blk, r, bidx, c0 = items[n]
                sb = n % 4
                mm(bank(sb)[0:kr, c0:512], KT[c][0:65, kc0:kc0 + kr], QT[c][0:65, q0 + c0:q0 + 512],
                   True, r < 0, KTkeys[c] + [("KTone", c), ("QT", c, sl)], [PK(sb)])
                if r >= 0:
                    mm(bank(sb)[:, c0:c0 + 128], ident, masks[:, r * 128:(r + 1) * 128], False, True,
                       ["cm", "masks"], [PK(sb)])

            def rest(n):
                c, first, last, kr, kc0, blk, r, bidx, c0 = items[n]
                sb = n % 4
                pbuf = Rbf(sb // 2, sb % 2)
                pkey = [("R", sb // 2, sb % 2)]
                act(pbuf[0:kr, c0:512], bank(sb)[0:kr, c0:512], AF.Exp, [PK(sb), "biasT"], pkey,
                    bias=biasT[0:kr, h * 36 + bidx:h * 36 + bidx + 1], scale=0.125)
                mm(bank(Ob[c])[:, c0:512], V3[0:kr, blk, :], pbuf[0:kr, c0:512], first, last,
                   pkey + [Vkeys[blk]], [PK(Ob[c])])
                mm(bank(Rb[c])[:, c0:512], ones[0:kr, :], pbuf[0:kr, c0:512], first, last,
                   pkey + ["cm"], [PK(Rb[c])])

            nit = len(items)
            qk(0)
            qk(1)
            for n in range(nit):
                if n + 2 < nit:
                    qk(n + 2)
                rest(n)
            y0, y1, t2, t3 = Rf[2], Rf[3], Rf[4], Rf[5]
            for c, yy in ((0, y0), (1, y1)):
                recip(t2, bank(Rb[c]), [PK(Rb[c])], RK(4))
                tt("dve", yy, bank(Ob[c]), t2, ALU.mult, [PK(Ob[c])] + RK(4), RK(2 + c))
            stt("dve", y0, y1, neglam[:, 0:1], y0, ALU.mult, ALU.add, RK(2) + RK(3) + ["neglam"], RK(2))
            tt("pool", t3, y0, y0, ALU.mult, RK(2), RK(5))
            mm(bank(0), onesf, t3, True, True, RK(5) + ["onesf"], [PK(0)])
            tsadd("dve", t2, bank(0), RMS_EPS, [PK(0)], RK(4))
            act(t2, t2, AF.Ln, RK(4), RK(4))
            act(t2, t2, AF.Exp, RK(4), RK(4), scale=-0.5)
            tt("dve", y0, y0, t2, ALU.mult, RK(2) + RK(4), RK(2))
            stt("dve", B["YT"][u][:, q0:q0 + 512], y0, sg[:, 1:2], GT[:, q0:q0 + 512], ALU.mult, ALU.mult,
                RK(2) + ["sgs", ("GT", sl)], [("YT", u, sl, 0), ("YT", u, sl, 1)])

        def sb_slot(P, u, sl, B):
            KTs, QTs, V3, GT, KTkeys, Vkeys = B["KTa"], B["QTa"], B["V3"], B["GT"], B["KTkeys"], B["Vkeys"]
            s = 2 * P + sl
            q0 = sl * 512
            gmax = 8 * s + 7
            glist = list(range(gmax, -1, -1)) + [-1]
            nst = len(glist)

            class St:
                pass
            sts = []
            for hh in range(2):
                o = St()
                tb = 9 * hh
                o.hh = hh
                o.p0, o.p1 = 64 * hh, 64 * hh + 64
                o.e, o.ek = [Rf[tb], Rf[tb + 1]], [RK(tb), RK(tb + 1)]
                o.x, o.xk = [Rf[tb + 2], Rf[tb + 3]], [RK(tb + 2), RK(tb + 3)]
                o.S, o.Sk = Rf[tb + 4], RK(tb + 4)
                o.sp, o.spk = [Rbf(tb + 5, 0), Rbf(tb + 5, 1)], [RKh(tb + 5, 0), RKh(tb + 5, 1)]
                o.Sb = [Rbf(tb + 6, 0), Rbf(tb + 6, 1), Rbf(tb + 7, 0)]
                o.Sbk = [RKh(tb + 6, 0), RKh(tb + 6, 1), RKh(tb + 7, 0)]
                o.a, o.ak = [Rbf(tb + 8, 0), Rbf(tb + 8, 1)], [RKh(tb + 8, 0), RKh(tb + 8, 1)]
                o.zb = [2 * hh, 2 * hh + 1]
                o.tbk = 4 + hh
                o.ob = 6 + hh
                sts.append(o)
            for o in sts:
                mset("pool", o.S, 0.0, [], o.Sk)
                for t in range(3):
                    mset("pool", o.Sb[t], 0.0, [], o.Sbk[t])
                mm(bank(o.ob), negones, o.Sb[0], True, False, o.Sbk[0] + ["cm"], [PK(o.ob)])

            def geom(n):
                g = glist[n]
                if g < 0:
                    return 16, 0, 0, -1, 0
                r = g - 8 * s
                return 128, NMETA + 128 * g, 1 + g, r, (128 * (r // 2) if r >= 0 else 0)

            def stageA(n):
                kr, kc0, blk, r, c0 = geom(n)
                for o in sts:
                    zb = o.zb[n % 2]
                    mm(bank(zb)[0:kr, c0:512], KTs[o.p0:o.p1, kc0:kc0 + kr], QTs[o.p0:o.p1, q0 + c0:q0 + 512],
                       True, r < 0, KTkeys[0] + [("QT", 0, sl), ("KTone", 0)], [PK(zb)])
                    if r >= 0:
                        mm(bank(zb)[:, c0:c0 + 128], ident, masks[:, 1024 + r * 128:1024 + (r + 1) * 128],
                           False, True, ["cm", "masks"], [PK(zb)])
                for o in sts:
                    zb = o.zb[n % 2]
                    act(o.e[n % 2][0:kr, c0:512], bank(zb)[0:kr, c0:512], AF.Exp, [PK(zb)], o.ek[n % 2], scale=0.125)
                for o in sts:
                    act(o.sp[n % 2][0:kr, c0:512], o.e[n % 2][0:kr, c0:512], AF.Ln, o.ek[n % 2], o.spk[n % 2],
                        bias=1.0)
                if n + 1 < nst:
                    for o in sts:
                        tt("pool", o.Sb[(n + 1) % 3][:, c0:512], o.S[:, c0:512], o.sp[n % 2][:, c0:512],
                           ALU.add, o.Sk + o.spk[n % 2], o.Sbk[(n + 1) % 3])
                        tt("dve", o.S[:, c0:512], o.S[:, c0:512], o.sp[n % 2][:, c0:512],
                           ALU.add, o.Sk + o.spk[n % 2], o.Sk)

            def stageB(n):
                kr, kc0, blk, r, c0 = geom(n)
                last = n == nst - 1
                for o in sts:
                    mm(bank(o.tbk)[0:kr, c0:512], negtri[0:kr, 0:kr], o.sp[n % 2][0:kr, c0:512], True, False,
                       o.spk[n % 2] + ["cm"], [PK(o.tbk)])
                    mm(bank(o.tbk)[0:kr, c0:512], negones[:, 0:kr], o.Sb[n % 3][:, c0:512], False, True,
                       o.Sbk[n % 3] + ["cm"], [PK(o.tbk)])
                for o in sts:
                    act(o.x[n % 2][0:kr, c0:512], bank(o.tbk)[0:kr, c0:512], AF.Exp, [PK(o.tbk)], o.xk[n % 2])
                for o in sts:
                    tt("dve", o.a[n % 2][0:kr, c0:512], o.e[n % 2][0:kr, c0:512], o.x[n % 2][0:kr, c0:512], ALU.mult,
                       o.ek[n % 2] + o.xk[n % 2], o.ak[n % 2])
                for o in sts:
                    mm(bank(o.ob)[:, c0:512], V3[0:kr, blk, :], o.a[n % 2][0:kr, c0:512], False, last,
                       o.ak[n % 2] + [Vkeys[blk]], [PK(o.ob)])

            stageA(0)
            for n in range(nst):
                if n + 1 < nst:
                    stageA(n + 1)
                stageB(n)
            for o in sts:
                tt("dve", B["YT"][u][o.p0:o.p1, q0:q0 + 512], bank(o.ob)[o.p0:o.p1, :], GT[o.p0:o.p1, q0:q0 + 512],
                   ALU.mult, [PK(o.ob), ("GT", sl)], [("YT", u, sl, o.hh)])

        def tail(P, B):
            KTkeys, Vkeys = B["KTkeys"], B["Vkeys"]
            A.release(mark_unit)
            wtb = A.alloc(4096, BF16)
            wm3 = wtb[:, 0:2048].rearrange("p (c n) -> p c n", c=8)
            wd3 = wtb[:, 2048:3072].rearrange("p (u n) -> p u n", u=8)
            ws3 = wtb[:, 3072:4096].rearrange("p (u n) -> p u n", u=8)
            mT = A.alloc(8 * 1024, BF16).rearrange("p (c t) -> p c t", c=8)
            wob = A.alloc(8 * 1024, BF16)
            wo3 = wob.rearrange("p (c n) -> p c n", c=8)
            ytall = [("YT", u, sl, hh) for u in range(16) for sl in range(2) for hh in range(2)]
            free_keys = ytall + ["wb0", "wb1", ("KTone", 0), ("KTone", 1)] + KTkeys[0] + KTkeys[1] + Vkeys + \
                [("QT", c, sl) for c in range(2) for sl in range(2)] + [("GT", 0), ("GT", 1)]
            mset("dve", dummy[:, 4:5], 0.0, free_keys, ["YTall", "tailfree"] + free_keys)
            mset("pool", dummy[:, 5:6], 0.0, ["tailfree"], ["tailfree2"])
            dma("pool", "dw2", wob, wo, ["tailfree2"], ["wob"])
            load_gb()
            for cc in range(8):
                dma("pool", "dw1", wtb, wt[cc], ["tailfree2"], ["wtb"])
                for sl in range(2):
                    q0 = sl * 512
                    bd, bs, bmd, bms = 0, 1, 2, 3
                    for uu in range(8):
                        mm(bank(bd), wd3[:, uu, :], YT[uu][:, q0:q0 + 512], uu == 0, uu == 7, ["wtb", "YTall"], [PK(bd)])
                    for uu in range(8):
                        mm(bank(bs), ws3[:, uu, :], YT[8 + uu][:, q0:q0 + 512], uu == 0, uu == 7,
                           ["wtb", "YTall"], [PK(bs)])
                    for ch in range(8):
                        mm(bank(bmd), wm3[:, ch, 0:128], hqT[:, ch, q0:q0 + 512], ch == 0, ch == 7,
                           ["wtb", "hqTall"], [PK(bmd)])
                    for ch in range(8):
                        mm(bank(bms), wm3[:, ch, 128:256], hqT[:, ch, q0:q0 + 512], ch == 0, ch == 7,
                           ["wtb", "hqTall"], [PK(bms)])
                    sd, ss = Rf[6], Rf[7]
                    for (bm, sgt, rk, br) in ((bmd, sd, 6, 0), (bms, ss, 7, 1)):
                        act(sgt, bank(bm), AF.Exp, [PK(bm), "bg"], RK(rk),
                            bias=bg[:, br * 8 + cc:br * 8 + cc + 1], scale=-1.0)
                        tsadd("dve", sgt, sgt, 1.0, RK(rk), RK(rk))
                        recip(sgt, sgt, RK(rk), RK(rk))
                    tt("dve", sd, bank(bd), sd, ALU.mult, [PK(bd)] + RK(6), RK(6))
                    tt("dve", ss, bank(bs), ss, ALU.mult, [PK(bs)] + RK(7), RK(7))
                    tt("pool", mT[:, cc, q0:q0 + 512], sd, ss, ALU.add, RK(6) + RK(7), [("mT", cc, sl)])
            mkeys = [("mT", cc, sl) for cc in range(8) for sl in range(2)]
            for jb in range(8):
                i = jb % 2
                r0 = (8 * P + jb) * 128
                xa, xb2 = xbuf[i]
                load_x(i, xq[r0:r0 + 128, :], 128)
                ln_core(i, 128, 0, (xa, xb2), (RK(2 * i), RK(2 * i + 1)))
                for hf in range(2):
                    b = 4 + 2 * i + hf
                    for cc in range(8):
                        mm(bank(b), mT[:, cc, jb * 128:(jb + 1) * 128], wo3[:, cc, hf * 512:(hf + 1) * 512],
                           cc == 0, cc == 7, mkeys + ["wob"], [PK(b)])
                    xs = (xa, xb2)[hf]
                    stt("dve", xs, xs, DN_ALPHA, bank(b), ALU.mult, ALU.add, RK(2 * i + hf) + [PK(b)], RK(2 * i + hf))
                ln_core(i, 128, 2, (xa, xb2), (RK(2 * i), RK(2 * i + 1)))
                dma("sp", "do%d" % i, out[r0:r0 + 128, 0:512], xa, RK(2 * i), [])
                dma("sp", "do%d" % i, out[r0:r0 + 128, 512:1024], xb2, RK(2 * i + 1), [])
            tk = mkeys + ["wob", "wtb", "YTall", "hqTall"]
            mset("dve", dummy[:, 6:7], 0.0, tk, ["tailend"] + tk)

        for P in passes:
            nreal = 16 * (P + 1)
            ncols = NMETA + 128 * nreal
            A.release(mark_unit)
            B = {"nreal": nreal, "ncols": ncols, "YT": YT}
            B["order"] = list(units)
            B["wbs"] = []
            for t_ in range(2):
                w_ = A.alloc(8 * 512, BF16)
                B["wbs"].append((w_, w_.rearrange("p (c n) -> p c n", c=8), "wb%d" % t_))
            B["KTa"] = A.alloc(L, BF16)
            B["KTb"] = A.alloc(L, BF16)
            B["V3"] = A.alloc(33 * 128, BF16).rearrange("p (b n) -> p b n", b=33)
            B["QTa"] = A.alloc(1024, BF16)
            B["QTb"] = A.alloc(1024, BF16)
            B["GT"] = A.alloc(1024, F32)
            B["ktiles"] = [(a, min(512, ncols - a)) for a in range(0, ncols, 512)]
            B["KTkeys"] = [[("KT", c, a) for (a, n) in B["ktiles"]] for c in range(2)]
            B["Vkeys"] = [("V", b) for b in range(1 + nreal)]
            if P != passes[0]:
                allk = ["wb0", "wb1", ("KTone", 0), ("KTone", 1)] + B["KTkeys"][0] + B["KTkeys"][1] + B["Vkeys"] + \
                    [("QT", c, sl) for c in range(2) for sl in range(2)] + [("GT", 0), ("GT", 1)]
                allk += [("YT", u_, sl_, hh_) for u_ in range(16) for sl_ in range(2) for hh_ in range(2)]
                allk += [("hqT", k_) for k_ in range(8)]
                mset("dve", dummy[:, 7:8], 0.0, ["tailend"], allk)

            for jb in range(8):
                r0 = (8 * P + jb) * 128
                ln_block_T(xq[r0:r0 + 128, :], 128, hqT, jb * 128, [("hqT", jb), "hqTall"])
            join([("hqT", k) for k in range(8)], "hqTall", 2)

            for u in units:
                unit_proj(P, u, B)
                for sl in range(2):
                    if u < 8:
                        diff_slot(P, u, sl, B)
                    else:
                        sb_slot(P, u, sl, B)

            if stage == "U":
                if P == passes[0]:
                    for k, u in enumerate(units[:2]):
                        for hf in range(2):
                            tmp = Rf[2 * k + hf]
                            vcopy("dve", tmp, YT[u][:, hf * 512:(hf + 1) * 512],
                                  [("YT", u, hf, 0), ("YT", u, hf, 1)], RK(2 * k + hf))
                            dma("sp", "do0", dbg[:, k * 1024 + hf * 512:k * 1024 + hf * 512 + 512], tmp,
                                RK(2 * k + hf), [])
                continue
            tail(P, B)

        T.emit(nc, sems, block)
    return nc


def _bf16(a):
    return np.asarray(a, dtype=np.float32).astype(ml_dtypes.bfloat16)


def _const_tables(p):
    k = np.arange(128)[:, None]
    i = np.arange(128)[None, :]
    diagD = np.where(k > i, NEG, 0.0).astype(np.float32)
    diagS = np.where(k >= i, NEG, 0.0).astype(np.float32)
    full = np.full((128, 128), NEG, np.float32)
    zero = np.zeros((128, 128), np.float32)
    m = np.zeros((128, 2, 8, 128), np.float32)
    for r in range(8):
        for bi, dg in enumerate((diagD, diagS)):
            if p == 0:
                m[:, bi, r, :] = dg if r % 2 == 0 else full
            else:
                m[:, bi, r, :] = zero if r % 2 == 0 else dg
    cmask = _bf16(m.reshape(128, 2048))
    bias = np.zeros((128, 8, 36), np.float32)
    kk = np.arange(128, dtype=np.float64)
    for h in range(8):
        for idx in range(32):
            delta = idx - 24
            bias[:, h, idx] = SLOPES[h] * (128.0 * (delta - p) + kk)
        for s in range(4):
            bias[:, h, 32 + s] = SLOPES[h] * (kk - 16.0 - 128.0 * (8 * s + p))
    cbias = bias.reshape(128, 8 * 36)
    col = np.arange(512)
    pos = 256.0 * (col // 128) + (col % 128)
    caug = _bf16(np.stack([-8.0 * SLOPES[h] * pos for h in range(8)]))
    j = np.arange(128)[:, None]
    kc = np.arange(128)[None, :]
    ident = (j == kc).astype(np.float32)
    negtri = -(j >= kc).astype(np.float32)
    negones = -np.ones((128, 128), np.float32)
    ones = np.ones((128, 128), np.float32)
    cmat = _bf16(np.concatenate([ident, negtri, negones, ones], axis=1))
    return cmask, cbias, caug, cmat


def _pc(w):
    n = w.shape[1]
    return np.ascontiguousarray(w.reshape(8, 128, n).transpose(1, 0, 2).reshape(128, 8 * n))


def make_in_maps(inputs):
    x = np.asarray(inputs["x"], np.float32)
    w_in = np.asarray(inputs["w_in"], np.float32)[0]
    secs = np.split(w_in, 10, axis=1)
    units = []
    for u in range(16):
        base = 0 if u < 8 else 4
        j = u % 8
        cols = [secs[base + t][:, j * 128:(j + 1) * 128] for t in range(4)]
        units.append(_pc(np.concatenate(cols, axis=1)))
    wu = np.stack(units)
    wbd = np.asarray(inputs["w_br_diff"], np.float32)[0]
    wbs = np.asarray(inputs["w_br_sb"], np.float32)[0]
    wt = np.zeros((8, 128, 4096), np.float32)
    for cc in range(8):
        cs = slice(cc * 128, (cc + 1) * 128)
        mcols = np.concatenate([secs[8][:, cs], secs[9][:, cs]], axis=1)
        wt[cc, :, 0:2048] = _pc(mcols)
        wt[cc, :, 2048:3072] = _pc(wbd[:, cs])
        wt[cc, :, 3072:4096] = _pc(wbs[:, cs])
    wo = _pc(np.asarray(inputs["w_out"], np.float32)[0])
    lnv = np.stack([np.asarray(inputs["emb_ln_g"], np.float32), np.asarray(inputs["emb_ln_b"], np.float32),
                    np.asarray(inputs["ln_g"], np.float32)[0], np.asarray(inputs["ln_b"], np.float32)[0]])
    bgate = np.ascontiguousarray(
        np.asarray(inputs["b_gate"], np.float32)[0].reshape(2, 8, 128).transpose(2, 0, 1).reshape(128, 16))
    lamv = np.asarray(inputs["diff_lambda"], np.float32)[0].reshape(1, 256)
    subg = np.asarray(inputs["diff_subln_g"], np.float32)[0].reshape(128, 1)
    meta = np.asarray(inputs["meta_tokens"], np.float32)
    maps = []
    for c in range(NCORES):
        b, p = c // 2, c % 2
        cmask, cbias, caug, cmat = _const_tables(p)
        xq = np.ascontiguousarray(x[b].reshape(16, 2, 128, D)[:, p].reshape(SEQ // 2, D))
        maps.append({"xf": np.ascontiguousarray(x[b]), "xq": xq, "meta": meta, "lnv": lnv, "wu": wu,
                     "wt": wt, "wo": wo, "bgate": bgate, "lamv": lamv, "subg": subg,
                     "cmask": cmask, "cbias": cbias, "caug": caug, "cmat": cmat})
    return maps


def kernel(**inputs):
    nc = build_program("full")
    maps = make_in_maps(inputs)
    res = run_bass_kernel_spmd(nc, maps, core_ids=list(range(NCORES)))
    outp = np.zeros((4, SEQ, D), np.float32)
    for c in range(NCORES):
        b, p = c // 2, c % 2
        outp[b].reshape(16, 2, 128, D)[:, p] = np.asarray(res.results[c]["out"]).reshape(16, 128, D)
    return outp
```
